# Optimizing a Trainium2 kernel written in Bass

```python
import jax
import jax.numpy as jnp
from jax import lax
import numpy as np

D_MODEL = 1024
BATCH = 8
SEQ = 2048
DEPTH = 4

GRID_W = 64
CTX_LEN = 256
D_MIX = D_MODEL
N_MIXERS = 4
D_GROUP = D_MIX // N_MIXERS
CONV_W = 3
HEAD_DIM = 64
N_Q_HEADS = D_GROUP // HEAD_DIM
N_KV_HEADS = 2
Q_PER_KV = N_Q_HEADS // N_KV_HEADS
WINDOW = 128
BLOCK = 128
ROPE_THETA = 10000.0
CHUNK = 128
N_SGU_GROUPS = 4
SGU_GROUP_DIM = D_GROUP // N_SGU_GROUPS
POOL_WINDOWS = (2, 4, 8, 16)
POOL_GROUP_DIM = D_GROUP // len(POOL_WINDOWS)
D_FF = 4 * D_MODEL
EPS = 1e-6

A_COLS = 3 * D_GROUP
Q_COLS = N_Q_HEADS * HEAD_DIM
KV_COLS = N_KV_HEADS * HEAD_DIM
B_COLS = Q_COLS + 2 * KV_COLS
C_COLS = 2 * D_GROUP
D_COLS = D_GROUP
B_OFF = A_COLS
KV_OFF = B_OFF + Q_COLS
C_OFF = B_OFF + B_COLS
D_OFF = C_OFF + C_COLS
D_PROJ = D_OFF + D_COLS

kernel_name = "hybrid_parallel_group_diffusion_trunk"


def rms_norm(x, g):
    xf = x.astype(jnp.float32)
    y = xf * lax.rsqrt(jnp.mean(xf * xf, axis=-1, keepdims=True) + EPS)
    return (y * g.astype(jnp.float32)).astype(x.dtype)


def axial_rope_tables(length):
    rows = length // GRID_W
    row = jnp.repeat(jnp.arange(rows), GRID_W).astype(jnp.float32)
    col = jnp.tile(jnp.arange(GRID_W), rows).astype(jnp.float32)
    n_freq = HEAD_DIM // 4
    inv = ROPE_THETA ** (-jnp.arange(n_freq, dtype=jnp.float32) / n_freq)
    ang_r = row[:, None] * inv[None, :]
    ang_c = col[:, None] * inv[None, :]
    ang = jnp.concatenate([ang_r, ang_r, ang_c, ang_c], axis=-1)
    return jnp.cos(ang), jnp.sin(ang)


def apply_rope(x, cos, sin):
    bshape = (cos.shape[0],) + (1,) * (x.ndim - 3) + (cos.shape[1],)
    xs = x.reshape(x.shape[:-1] + (2, 2, HEAD_DIM // 4))
    rot = jnp.stack([-xs[..., 1, :], xs[..., 0, :]], axis=-2).reshape(x.shape)
    return (x * cos.reshape(bshape) + rot * sin.reshape(bshape)).astype(x.dtype)


def short_conv_mixer(p, conv_w):
    h, gate_b, gate_c = jnp.split(p, 3, axis=-1)
    z = gate_c * h
    zp = jnp.pad(z, ((0, 0), (1, 1), (0, 0)))
    y = conv_w[0] * zp[:, :-2] + conv_w[1] * zp[:, 1:-1] + conv_w[2] * zp[:, 2:]
    return gate_b * y


def split_kv(pkv):
    k = pkv[..., :KV_COLS].reshape(pkv.shape[:-1] + (N_KV_HEADS, HEAD_DIM))
    v = pkv[..., KV_COLS:].reshape(pkv.shape[:-1] + (N_KV_HEADS, HEAD_DIM))
    return k, v


def window_attention(q, k, v, k_ctx, v_ctx, sink):
    bsz, length = q.shape[0], q.shape[1]
    nb = length // BLOCK
    qb = q.reshape(bsz, nb, BLOCK, N_KV_HEADS, Q_PER_KV, HEAD_DIM)

    def band(t):
        tb = t.reshape(bsz, nb, BLOCK, N_KV_HEADS, HEAD_DIM)
        tp = jnp.pad(tb, ((0, 0), (1, 1), (0, 0), (0, 0), (0, 0)))
        return jnp.concatenate([tp[:, :-2], tp[:, 1:-1], tp[:, 2:]], axis=2)

    kb, vb = band(k), band(v)
    s_loc = jnp.einsum("bnqkgd,bnjkd->bnkgqj", qb, kb).astype(jnp.float32)
    r = jnp.arange(BLOCK)
    j = jnp.arange(3 * BLOCK)
    blk = jnp.arange(nb)
    rel = j[None, :] - BLOCK - r[:, None]
    kpos = blk[:, None] * BLOCK - BLOCK + j[None, :]
    mask = (jnp.abs(rel) <= WINDOW)[None, :, :] & ((kpos >= 0) & (kpos < length))[:, None, :]
    s_loc = jnp.where(mask[None, :, None, None, :, :], s_loc, -jnp.inf)
    s_ctx = jnp.einsum("bnqkgd,bckd->bnkgqc", qb, k_ctx).astype(jnp.float32)
    s_sink = jnp.broadcast_to(sink.astype(jnp.float32).reshape(1, 1, N_KV_HEADS, Q_PER_KV, 1, 1),
                              s_loc.shape[:-1] + (1,))
    probs = jax.nn.softmax(jnp.concatenate([s_loc, s_ctx, s_sink], axis=-1), axis=-1).astype(v.dtype)
    n_loc = 3 * BLOCK
    n_ctx = k_ctx.shape[1]
    o = (jnp.einsum("bnkgqj,bnjkd->bnqkgd", probs[..., :n_loc], vb)
         + jnp.einsum("bnkgqc,bckd->bnqkgd", probs[..., n_loc:n_loc + n_ctx], v_ctx))
    return o.reshape(bsz, length, Q_COLS)


def context_attention(q, k, v, sink):
    s = jnp.einsum("bqkgd,bckd->bkgqc", q, k).astype(jnp.float32)
    s_sink = jnp.broadcast_to(sink.astype(jnp.float32).reshape(1, N_KV_HEADS, Q_PER_KV, 1, 1),
                              s.shape[:-1] + (1,))
    probs = jax.nn.softmax(jnp.concatenate([s, s_sink], axis=-1), axis=-1).astype(v.dtype)
    o = jnp.einsum("bkgqc,bckd->bqkgd", probs[..., :-1], v)
    return o.reshape(q.shape[0], q.shape[1], Q_COLS)


def chunk_sgu_mixer(p, sgu_norm, w_s, b_s):
    u, v = jnp.split(p, 2, axis=-1)
    v = rms_norm(v, sgu_norm)
    bsz, length = v.shape[0], v.shape[1]
    vc = v.reshape(bsz, length // CHUNK, CHUNK, N_SGU_GROUPS, SGU_GROUP_DIM)
    z = jnp.einsum("gpq,bnqgc->bnpgc", w_s, vc) + b_s.T[:, :, None]
    return u * z.reshape(u.shape)


def centred_mean(x, w):
    length = x.shape[1]
    xf = x.astype(jnp.float32)
    cs = jnp.concatenate([jnp.zeros_like(xf[:, :1]), jnp.cumsum(xf, axis=1)], axis=1)
    t = jnp.arange(length)
    lo = jnp.clip(t - w // 2, 0, length)
    hi = jnp.clip(t + w // 2, 0, length)
    cnt = (hi - lo).astype(jnp.float32)
    return ((cs[:, hi] - cs[:, lo]) / cnt[None, :, None]).astype(x.dtype)


def pool_mixer(p, w_pool, pool_scale):
    groups = jnp.split(p, len(POOL_WINDOWS), axis=-1)
    d = jnp.stack([centred_mean(g, w) - g for g, w in zip(groups, POOL_WINDOWS)], axis=-2)
    y = jnp.einsum("btgc,gcd->btgd", d, w_pool)
    return y.reshape(p.shape) * pool_scale


def local_mixers(p, conv_w, sgu_norm, w_sgu, b_sgu, w_pool, pool_scale):
    y_a = short_conv_mixer(p[..., :A_COLS], conv_w)
    y_c = chunk_sgu_mixer(p[..., C_OFF:D_OFF], sgu_norm, w_sgu, b_sgu)
    y_d = pool_mixer(p[..., D_OFF:], w_pool, pool_scale)
    return y_a, y_c, y_d


def setup_inputs(seed: int = 0) -> dict:
    key = jax.random.key(seed)
    ks = jax.random.split(key, 24)
    f32 = jnp.float32

    def nrm(k, shape, scale):
        return jax.random.normal(k, shape, f32) * scale

    def gain(k, shape):
        return 1.0 + 0.1 * jax.random.normal(k, shape, f32)

    return {
        "x": nrm(ks[0], (BATCH, SEQ, D_MODEL), 1.0),
        "c": nrm(ks[1], (BATCH, D_MODEL), 1.0),
        "ctx": nrm(ks[2], (BATCH, CTX_LEN, D_MODEL), 1.0),
        "c_ctx": nrm(ks[3], (D_MODEL,), 1.0),
        "norm_mix": gain(ks[4], (DEPTH, D_MODEL)),
        "norm_ff": gain(ks[5], (DEPTH, D_MODEL)),
        "w_ada": nrm(ks[6], (DEPTH, D_MODEL, 6 * D_MODEL), 0.5 * D_MODEL ** -0.5),
        "b_ada": nrm(ks[7], (DEPTH, 6 * D_MODEL), 0.01),
        "w_in": nrm(ks[8], (DEPTH, D_MODEL, D_PROJ), D_MODEL ** -0.5),
        "w_out": nrm(ks[9], (DEPTH, D_MIX, D_MODEL), D_MIX ** -0.5),
        "conv_w": nrm(ks[10], (DEPTH, CONV_W, D_GROUP), CONV_W ** -0.5),
        "q_norm": gain(ks[11], (DEPTH, HEAD_DIM)),
        "k_norm": gain(ks[12], (DEPTH, HEAD_DIM)),
        "sink": nrm(ks[13], (DEPTH, N_Q_HEADS), 1.0),
        "sgu_norm": gain(ks[14], (DEPTH, D_GROUP)),
        "w_sgu": nrm(ks[15], (DEPTH, N_SGU_GROUPS, CHUNK, CHUNK), CHUNK ** -0.5),
        "b_sgu": gain(ks[16], (DEPTH, N_SGU_GROUPS, CHUNK)),
        "w_pool": nrm(ks[17], (DEPTH, len(POOL_WINDOWS), POOL_GROUP_DIM, POOL_GROUP_DIM), POOL_GROUP_DIM ** -0.5),
        "pool_scale": gain(ks[18], (DEPTH, D_GROUP)),
        "w_ff1": nrm(ks[19], (DEPTH, D_MODEL, D_FF), D_MODEL ** -0.5),
        "w_ff2": nrm(ks[20], (DEPTH, D_FF, D_MODEL), D_FF ** -0.5),
    }


def reference(x, c, ctx, c_ctx, norm_mix, norm_ff, w_ada, b_ada, w_in, w_out, conv_w, q_norm, k_norm,
              sink, sgu_norm, w_sgu, b_sgu, w_pool, pool_scale, w_ff1, w_ff2):
    length = x.shape[1]
    cos, sin = axial_rope_tables(length)
    silu_c = jax.nn.silu(c)
    silu_cc = jax.nn.silu(c_ctx)
    h_lat, h_ctx = x, ctx
    for l in range(DEPTH):
        last = l == DEPTH - 1
        mod = (silu_c @ w_ada[l] + b_ada[l])[:, None, :]
        sh1, sc1, g1, sh2, sc2, g2 = jnp.split(mod, 6, axis=-1)
        if last:
            modc = silu_cc @ w_ada[l][:, :2 * D_MODEL] + b_ada[l][:2 * D_MODEL]
            csh1, csc1 = jnp.split(modc, 2, axis=-1)
        else:
            modc = silu_cc @ w_ada[l] + b_ada[l]
            csh1, csc1, cg1, csh2, csc2, cg2 = jnp.split(modc, 6, axis=-1)

        a_lat = rms_norm(h_lat, norm_mix[l]) * (1 + sc1) + sh1
        a_ctx = rms_norm(h_ctx, norm_mix[l]) * (1 + csc1) + csh1
        p_lat = a_lat @ w_in[l]
        if last:
            pkv_ctx = a_ctx @ w_in[l][:, KV_OFF:C_OFF]
        else:
            p_ctx = a_ctx @ w_in[l]
            pkv_ctx = p_ctx[..., KV_OFF:C_OFF]
        k_c, v_c = split_kv(pkv_ctx)
        k_c = rms_norm(k_c, k_norm[l])

        q_l = p_lat[..., B_OFF:KV_OFF].reshape(p_lat.shape[:2] + (N_KV_HEADS, Q_PER_KV, HEAD_DIM))
        k_l, v_l = split_kv(p_lat[..., KV_OFF:C_OFF])
        q_l = apply_rope(rms_norm(q_l, q_norm[l]), cos, sin) * (HEAD_DIM ** -0.5)
        k_l = apply_rope(rms_norm(k_l, k_norm[l]), cos, sin)
        y_b = window_attention(q_l, k_l, v_l, k_c, v_c, sink[l])
        y_a, y_c, y_d = local_mixers(p_lat, conv_w[l], sgu_norm[l], w_sgu[l], b_sgu[l], w_pool[l], pool_scale[l])
        y_lat = jnp.concatenate([y_a, y_b, y_c, y_d], axis=-1) @ w_out[l]
        h_lat = h_lat + g1 * y_lat

        if not last:
            q_c = p_ctx[..., B_OFF:KV_OFF].reshape(p_ctx.shape[:2] + (N_KV_HEADS, Q_PER_KV, HEAD_DIM))
            q_c = rms_norm(q_c, q_norm[l]) * (HEAD_DIM ** -0.5)
            yc_b = context_attention(q_c, k_c, v_c, sink[l])
            yc_a, yc_c, yc_d = local_mixers(p_ctx, conv_w[l], sgu_norm[l], w_sgu[l], b_sgu[l], w_pool[l], pool_scale[l])
            y_ctx = jnp.concatenate([yc_a, yc_b, yc_c, yc_d], axis=-1) @ w_out[l]
            h_ctx = h_ctx + cg1 * y_ctx

        f_lat = rms_norm(h_lat, norm_ff[l]) * (1 + sc2) + sh2
        h_lat = h_lat + g2 * (jnp.square(jax.nn.relu(f_lat @ w_ff1[l])) @ w_ff2[l])
        if not last:
            f_ctx = rms_norm(h_ctx, norm_ff[l]) * (1 + csc2) + csh2
            h_ctx = h_ctx + cg2 * (jnp.square(jax.nn.relu(f_ctx @ w_ff1[l])) @ w_ff2[l])
    return h_lat
```

```python
import numpy as np
import concourse.bass as bass
import concourse.mybir as mybir
from concourse.bass_utils import run_bass_kernel_spmd

F32 = mybir.dt.float32
BF16 = mybir.dt.bfloat16
AF = mybir.ActivationFunctionType
ALU = mybir.AluOpType
AX = mybir.AxisListType

D = 1024
T = 2304
TL = 2048
TC = 256
DEPTH = 4
EPS = 1e-6
TILES = [(0, 512), (512, 512), (1024, 512), (1536, 512), (2048, 256)]
NSLOT = 3
NDS = 24
NDS_SW = 16
NPP = 80

ENGS = ("pe", "act", "dve", "pool", "sp")


class Op:
    __slots__ = ("eng", "fn", "dma", "deps", "marked", "rank", "dsem", "dval", "prev_dval")

    def __init__(self, eng, fn, dma):
        self.eng = eng
        self.fn = fn
        self.dma = dma
        self.deps = set()
        self.marked = False
        self.rank = 0
        self.dsem = -1
        self.dval = 0
        self.prev_dval = 0


class Prog:
    def __init__(self):
        self.ops = {e: [] for e in ENGS}
        self.lastw = {}
        self.readers = {}
        self.ndma = 0
        self.ndma_sw = 0
        self.final = []

    def add(self, eng, fn, reads=(), writes=(), dma=False):
        op = Op(eng, fn, dma)
        deps = set()
        for k in reads:
            w = self.lastw.get(k)
            if w is not None:
                deps.add(w)
        for k in writes:
            w = self.lastw.get(k)
            if w is not None:
                deps.add(w)
            for r in self.readers.get(k, ()):
                deps.add(r)
        for k in reads:
            self.readers.setdefault(k, []).append(op)
        for k in writes:
            self.lastw[k] = op
            self.readers[k] = []
        deps.discard(op)
        for d in deps:
            if d.dma:
                op.deps.add(d)
            elif d.eng == eng and eng == "pe" and not dma:
                continue
            else:
                op.deps.add(d)
                d.marked = True
        if dma:
            if eng == "pool":
                n = self.ndma_sw
                self.ndma_sw += 1
                base, cntp = 0, NDS_SW
            else:
                n = self.ndma
                self.ndma += 1
                base, cntp = NDS_SW, NDS - NDS_SW
            op.dsem = base + n % cntp
            op.dval = 16 * (n // cntp + 1)
            op.prev_dval = 16 * (n // cntp)
        self.ops[eng].append(op)
        return op

    def emit(self, handles, esem, dsems):
        for e in ENGS:
            r = 0
            for op in self.ops[e]:
                if not op.dma and op.marked:
                    r += 1
                    op.rank = r
        for e in ENGS:
            h = handles[e]
            known = {}
            for op in self.ops[e]:
                need = {}
                for d in op.deps:
                    if d.dma:
                        key = ("d", d.dsem)
                        val = d.dval
                    else:
                        key = ("e", d.eng)
                        val = d.rank
                    if need.get(key, 0) < val:
                        need[key] = val
                if op.dma and op.prev_dval > 0:
                    key = ("d", op.dsem)
                    if need.get(key, 0) < op.prev_dval:
                        need[key] = op.prev_dval
                for key, val in need.items():
                    if known.get(key, 0) >= val:
                        continue
                    known[key] = val
                    sem = dsems[key[1]] if key[0] == "d" else esem[key[1]]
                    h.wait_ge(sem, val)
                ins = op.fn(h)
                if op.dma:
                    ins.then_inc(dsems[op.dsem], 16)
                elif op.marked:
                    ins.then_inc(esem[e], 1)
            for (fe, dop) in self.final:
                if fe == e:
                    h.wait_ge(dsems[dop.dsem], dop.dval)


def build(L, dbg=None):
    dbg = dbg or {}
    MIX = dbg.get('mix', 'BACD')
    DOFFN = dbg.get('ffn', True)
    nc = bass.Bass("TRN2", target_bir_lowering=False)
    dt_in = lambda name, shape: nc.dram_tensor(name, shape, F32, kind="ExternalInput").ap()
    hT0 = dt_in("hT0", [D, T])
    cvec = dt_in("cvec", [128, 16])
    w_ada = dt_in("w_ada", [L, D, 6 * D])
    w_in = dt_in("w_in", [L, D, 2048])
    w_out = dt_in("w_out", [L, D, D])
    w_ff1 = dt_in("w_ff1", [L, D, 4 * D])
    w_ff2 = dt_in("w_ff2", [L, 4 * D, D])
    ppd = dt_in("pp", [L, 128, NPP])
    sgnd = dt_in("sgn", [L, 128, 256])
    bstd = dt_in("bst", [L, 128, 256])
    wstd = dt_in("wst", [L, 128, 512])
    wpd = dt_in("wpool", [L, 4, 64, 64])
    cosd = dt_in("cosd", [128, TL])
    sind = dt_in("sind", [128, TL])
    cstd = dt_in("cstd", [128, 1024])
    rced = dt_in("rced", [128, 36])
    outT = nc.dram_tensor("outT", [D, T], F32, kind="ExternalOutput").ap()

    P = Prog()
    SW = 2336
    PHW = 15168

    import contextlib
    with contextlib.ExitStack() as es:
        sb = lambda name, shape, dt: es.enter_context(nc.sbuf_tensor("s_" + name, shape, dt))
        hT = sb("hT", [128, 8, T], F32)
        aT = sb("aT", [128, 8, T], BF16)
        wr = sb("wr", [128, NSLOT, 4096], BF16)
        cosT = sb("cosT", [128, TL], BF16)
        sinT = sb("sinT", [128, TL], BF16)
        cst = sb("cst", [128, 1024], BF16)
        pp = sb("pp", [128, NPP], F32)
        sgn = sb("sgn", [128, 256], F32)
        bst = sb("bst", [128, 2, 128], F32)
        wst = sb("wst", [128, 4, 128], BF16)
        wpb = sb("wpb", [128, 2, 128], BF16)
        cv = sb("cv", [128, 16], F32)
        svec = sb("svec", [128, 8, 2], BF16)
        modT = sb("modT", [128, 2, 48, 2], F32)
        der = sb("der", [128, 2, 2, 6, 8], F32)
        sm = sb("sm", [128, 16], F32)
        rce = sb("rce", [128, 36], F32)
        ssq = sb("ssq", [128, 12], F32)
        epsc = sb("epsc", [128, 4], F32)
        ph = sb("ph", [128, PHW], F32)
        ps = es.enter_context(nc.psum_tensor("ps", [128, 8, 512], F32))
        esem = {e: es.enter_context(nc.semaphore("sem_" + e)) for e in ENGS}
        dsems = [es.enter_context(nc.semaphore("dsem%d" % i)) for i in range(NDS)]

        S0 = ph[:, 0:SW]
        S1 = ph[:, SW:2 * SW]
        R1o = 2 * SW
        qT = ph[:, R1o:R1o + 2304].bitcast(BF16).rearrange("p (c t) -> p c t", c=2)
        kT = ph[:, R1o + 2304:R1o + 3456].bitcast(BF16)
        Vt = ph[:, R1o + 3456:R1o + 4608].bitcast(BF16).rearrange("p (b c) -> p b c", b=18)
        S2 = ph[:, R1o:R1o + SW]
        dT = ph[:, R1o + SW:R1o + SW + SW // 2].bitcast(BF16)
        Yo = R1o + 4608
        Y = ph[:, Yo:Yo + 2304].bitcast(BF16).rearrange("p (c t) -> p c t", c=2)
        TPo = Yo + 2304
        TP = [ph[:, TPo + i * 512:TPo + (i + 1) * 512] for i in range(5)]
        SQo = TPo + 5 * 512
        SQ = [ph[:, SQo + i * 256:SQo + (i + 1) * 256].bitcast(BF16) for i in range(2)]
        PTo = SQo + 2 * 256
        PT = [ph[:, PTo + i * 128:PTo + (i + 1) * 128].bitcast(BF16) for i in range(2)]
        VNo = PTo + 2 * 128
        VN = [ph[:, VNo + i * 128:VNo + (i + 1) * 128].bitcast(BF16) for i in range(2)]
        assert VNo + 256 <= PHW, VNo + 256
        gT = ph[:, 0:4608].bitcast(BF16).rearrange("p (c t) -> p c t", c=4)
        RT = [TP[3], TP[4]]

        ones = cst[:, 0:128]
        bones = cst[:, 128:256]
        Rm = cst[:, 256:384]
        ident = cst[:, 384:512]
        mprev = cst[:, 512:768]
        mnext = cst[:, 768:1024]

        cnt = {"dense": 0, "misc": 0, "tok": 0, "slot": 0}

        def dense_bank():
            b = cnt["dense"] % 4
            cnt["dense"] += 1
            return b

        def misc_bank():
            b = 4 + cnt["misc"] % 2
            cnt["misc"] += 1
            return b

        def tok_bank():
            b = 6 + cnt["tok"] % 2
            cnt["tok"] += 1
            return b

        free_slots = list(range(NSLOT))
        cfg = {"nt": 5}

        def alloc_slot():
            assert free_slots, "weight ring exhausted"
            return free_slots.pop(0)

        def release_slot(sl):
            assert sl not in free_slots
            free_slots.append(sl)

        def pk(b):
            return ("ps", b)

        for k in range(8):
            P.add("sp", lambda h, k=k: h.dma_start(out=hT[:, k, :], in_=hT0[k * 128:(k + 1) * 128, :]),
                  writes=[("hT", k, ti) for ti in range(5)], dma=True)
        P.add("sp", lambda h: h.dma_start(out=cv[:], in_=cvec[:, :]), writes=["cv"], dma=True)
        P.add("sp", lambda h: h.dma_start(out=rce[:], in_=rced[:, :]), writes=["rce"], dma=True)
        P.add("pool", lambda h: h.dma_start(out=cosT[:], in_=cosd[:, :]), writes=["cos"], dma=True)
        P.add("pool", lambda h: h.dma_start(out=sinT[:], in_=sind[:, :]), writes=["sin"], dma=True)
        P.add("pool", lambda h: h.dma_start(out=cst[:], in_=cstd[:, :]), writes=["cst"], dma=True)
        P.add("act", lambda h: h.activation(out=svec[:].rearrange("p k i -> p (k i)"), in_=cv[:], func=AF.Silu),
              reads=["cv"], writes=["svec"])
        P.add("dve", lambda h: h.memset(wpb[:].rearrange("p a b -> p (a b)"), 0.0), writes=["wpb"])
        P.add("dve", lambda h: h.memset(epsc[:, 0:1], float(D * EPS)), writes=["epsc"])
        P.add("dve", lambda h: h.memset(epsc[:, 1:2], float(64 * EPS)), writes=["epsc"])
        P.add("dve", lambda h: h.memset(epsc[:, 2:3], float(256 * EPS)), writes=["epsc"])

        def load_slot(src_ap, s, shape3):
            a, b = shape3
            dst = wr[:, s, :].rearrange("p (a b) -> p a b", a=a)
            return P.add("pool", lambda h: h.dma_start(out=dst, in_=src_ap), writes=[("wr", s)], dma=True)

        def load_params(l):
            P.add("sp", lambda h: h.dma_start(out=pp[:], in_=ppd[l]), writes=["pp"], dma=True)
            P.add("sp", lambda h: h.dma_start(out=sgn[:], in_=sgnd[l]), writes=["sgn"], dma=True)
            P.add("sp", lambda h: h.dma_start(out=bst[:].rearrange("p a b -> p (a b)"), in_=bstd[l]),
                  writes=["bst"], dma=True)
            P.add("pool", lambda h: h.dma_start(out=wst[:].rearrange("p a b -> p (a b)"), in_=wstd[l]),
                  writes=["wst"], dma=True)
            for j in range(2):
                P.add("pool", lambda h, j=j: h.dma_start(out=wpb[0:64, j, 0:64], in_=wpd[l, 2 * j]),
                      writes=["wpb"], dma=True)
                P.add("pool", lambda h, j=j: h.dma_start(out=wpb[64:128, j, 64:128], in_=wpd[l, 2 * j + 1]),
                      writes=["wpb"], dma=True)

        def mod_load(l, cg):
            s = alloc_slot()
            src = w_ada[l].rearrange("(k p) n -> p k n", p=128)[:, :, cg * 512:(cg + 1) * 512]
            load_slot(src, s, (8, 512))
            return s

        def mod_mm(s, cg):
            for cc in range(4):
                col = (cg * 4 + cc) * 2
                for k in range(8):
                    P.add("pe", lambda h, s=s, cc=cc, k=k, col=col: h.matmul(
                        ps[:, 5, col:col + 2], lhsT=wr[:, s, k * 512 + cc * 128:k * 512 + (cc + 1) * 128],
                        rhs=svec[:, k, :], start=(k == 0), stop=(k == 7)),
                        reads=[("wr", s), "svec"], writes=[pk(5)])
            release_slot(s)

        def mod_finish(par):
            P.add("dve", lambda h: h.tensor_tensor(
                out=modT[:, par, :, :], in0=ps[:, 5, 0:96].rearrange("p (c i) -> p c i", i=2),
                in1=pp[:, 0:48].unsqueeze(2).broadcast_to([128, 48, 2]), op=ALU.add),
                reads=["pp"], writes=[pk(5), ("modT", par)])
            for i in range(2):
                for (w, scc, shc, gc, nrm) in ((0, 1, 0, 2, 48), (1, 4, 3, 5, 56)):
                    P.add("dve", lambda h, i=i, w=w, scc=scc, nrm=nrm: h.scalar_tensor_tensor(
                        out=der[:, par, i, 3 * w + 0, :], in0=modT[:, par, scc * 8:scc * 8 + 8, i], scalar=1.0,
                        in1=pp[:, nrm:nrm + 8], op0=ALU.add, op1=ALU.mult),
                        reads=[("modT", par), "pp"], writes=[("der", par, i, w, 0)])
                    P.add("dve", lambda h, i=i, w=w: h.tensor_scalar(
                        out=der[:, par, i, 3 * w + 0, :], in0=der[:, par, i, 3 * w + 0, :], scalar1=float(np.sqrt(D)),
                        scalar2=None, op0=ALU.mult),
                        reads=[("der", par, i, w, 0)], writes=[("der", par, i, w, 0)])
                    P.add("dve", lambda h, i=i, w=w, shc=shc: h.tensor_copy(
                        out=der[:, par, i, 3 * w + 1, :], in_=modT[:, par, shc * 8:shc * 8 + 8, i]),
                        reads=[("modT", par)], writes=[("der", par, i, w, 1)])
                    P.add("dve", lambda h, i=i, w=w, gc=gc: h.tensor_copy(
                        out=der[:, par, i, 3 * w + 2, :], in_=modT[:, par, gc * 8:gc * 8 + 8, i]),
                        reads=[("modT", par)], writes=[("der", par, i, w, 2)])

        def layer_small(l):
            P.add("dve", lambda h: h.tensor_scalar(out=sm[:, 0:2], in0=pp[:, 64:66], scalar1=8.0, scalar2=None,
                                                   op0=ALU.mult), reads=["pp"], writes=["sm01"])
            P.add("act", lambda h: h.activation(out=sm[:, 2:4], in_=pp[:, 66:68], func=AF.Exp),
                  reads=["pp"], writes=["sm23"])
            P.add("act", lambda h: h.activation(out=sm[:, 12:16], in_=pp[:, 76:80], func=AF.Exp),
                  reads=["pp"], writes=["sm12"])
            P.add("dve", lambda h: h.tensor_scalar(out=sgn[:], in0=sgn[:], scalar1=16.0, scalar2=None,
                                                   op0=ALU.mult), reads=["sgn"], writes=["sgn"])

        def tile_i(ti):
            return 1 if ti == 4 else 0

        def norm_phase(par, w, nt=5):
            mbs = {}

            SQN = [SQ[0], SQ[1]] + [ph[:, R1o + i * 256:R1o + (i + 1) * 256].bitcast(BF16) for i in range(4)]

            def sq_one(ti, k):
                t0, tn = TILES[ti]
                if k == 0:
                    mbs[ti] = misc_bank()
                mb = mbs[ti]
                sqi = (ti * 8 + k) % 6
                sq = SQN[sqi]
                if k in (1, 3):
                    P.add("pool", lambda h: h.tensor_tensor(
                        out=sq[:, 0:tn], in0=hT[:, k, t0:t0 + tn], in1=hT[:, k, t0:t0 + tn], op=ALU.mult),
                        reads=[("hT", k, ti)], writes=[("SQ", sqi)])
                elif k == 99:
                    P.add("dve", lambda h: h.tensor_tensor(
                        out=sq[:, 0:tn], in0=hT[:, k, t0:t0 + tn], in1=hT[:, k, t0:t0 + tn], op=ALU.mult),
                        reads=[("hT", k, ti)], writes=[("SQ", sqi)])
                else:
                    P.add("act", lambda h: h.activation(
                        out=sq[:, 0:tn], in_=hT[:, k, t0:t0 + tn], func=AF.Square),
                        reads=[("hT", k, ti)], writes=[("SQ", sqi)])
                P.add("pe", lambda h: h.matmul(
                    ps[:, mb, 0:tn], lhsT=ones, rhs=sq[:, 0:tn], start=(k == 0), stop=(k == 7)),
                    reads=[("SQ", sqi), "cst"], writes=[pk(mb)])

            def rstd_part(ti):
                t0, tn = TILES[ti]
                mb = mbs[ti]
                rs = ti % 2
                P.add("act", lambda h: h.activation(
                    out=TP[rs][:, 0:tn], in_=ps[:, mb, 0:tn], func=AF.Ln, bias=epsc[:, 0:1], scale=1.0),
                    reads=["epsc"], writes=[pk(mb), ("TP", rs)])
                P.add("act", lambda h: h.activation(
                    out=TP[rs][:, 0:tn], in_=TP[rs][:, 0:tn], func=AF.Exp, scale=-0.5),
                    reads=[("TP", rs)], writes=[("TP", rs)])

            def apply_one(ti, k):
                t0, tn = TILES[ti]
                i = tile_i(ti)
                rs = ti % 2
                tp = 2 + k % 3
                eng = "pool" if k in (1, 3) else "dve"
                P.add(eng, lambda h: h.tensor_tensor(
                    out=TP[tp][:, 0:tn], in0=hT[:, k, t0:t0 + tn], in1=TP[rs][:, 0:tn], op=ALU.mult),
                    reads=[("hT", k, ti), ("TP", rs)], writes=[("TP", tp)])
                if k % 2 == 0:
                    P.add("act", lambda h: h.activation(
                        out=aT[:, k, t0:t0 + tn], in_=TP[tp][:, 0:tn], func=AF.Identity,
                        bias=der[:, par, i, 3 * w + 1, k:k + 1], scale=der[:, par, i, 3 * w + 0, k:k + 1]),
                        reads=[("TP", tp), ("der", par, i, w, 0), ("der", par, i, w, 1)], writes=[("aT", k, ti)])
                else:
                    P.add("dve", lambda h: h.tensor_scalar(
                        out=aT[:, k, t0:t0 + tn], in0=TP[tp][:, 0:tn],
                        scalar1=der[:, par, i, 3 * w + 0, k:k + 1], scalar2=der[:, par, i, 3 * w + 1, k:k + 1],
                        op0=ALU.mult, op1=ALU.add),
                        reads=[("TP", tp), ("der", par, i, w, 0), ("der", par, i, w, 1)], writes=[("aT", k, ti)])

            for k in range(8):
                sq_one(0, k)
            for ti in range(nt):
                rstd_part(ti)
                for k in range(8):
                    if ti + 1 < nt and k < 4:
                        sq_one(ti + 1, 2 * k)
                        sq_one(ti + 1, 2 * k + 1)
                    apply_one(ti, k)

        def mm8(b, s, c, ti):
            t0, tn = TILES[ti]
            for k in range(8):
                P.add("pe", lambda h, k=k: h.matmul(
                    ps[:, b, 0:tn], lhsT=wr[:, s, k * 512 + c * 128:k * 512 + (c + 1) * 128],
                    rhs=aT[:, k, t0:t0 + tn], start=(k == 0), stop=(k == 7)),
                    reads=[("wr", s), ("aT", k, ti)], writes=[pk(b)])

        def wout_partial(par, s, half, tile_outer=False):
            order = [(m, ti) for m in range(8) for ti in range(cfg["nt"])]
            if tile_outer:
                order = [(m, ti) for ti in range(cfg["nt"]) for m in range(8)]
            for (m, ti) in order:
                for (t0, tn) in (TILES[ti],):
                    i = tile_i(ti)
                    b = dense_bank()
                    for kk in range(2):
                        off = (half * 2 + kk) * 1024 + m * 128
                        P.add("pe", lambda h, kk=kk, off=off, b=b, t0=t0, tn=tn: h.matmul(
                            ps[:, b, 0:tn], lhsT=wr[:, s, off:off + 128], rhs=Y[:, kk, t0:t0 + tn],
                            start=(kk == 0), stop=(kk == 1)),
                            reads=[("wr", s), ("Y", kk, ti)], writes=[pk(b)])
                    cnt["wp"] = cnt.get("wp", 0) + 1
                    if cnt["wp"] % 4 == 0:
                        P.add("act", lambda h, m=m, b=b, tn=tn, i=i: h.activation(
                            out=TP[1][:, 0:tn], in_=ps[:, b, 0:tn], func=AF.Identity, scale=der[:, par, i, 2, m:m + 1]),
                            reads=[("der", par, i, 0, 2)], writes=[pk(b), ("TP", 1)])
                        P.add("pool", lambda h, m=m, t0=t0, tn=tn: h.tensor_tensor(
                            out=hT[:, m, t0:t0 + tn], in0=hT[:, m, t0:t0 + tn], in1=TP[1][:, 0:tn], op=ALU.add),
                            reads=[("TP", 1)], writes=[("hT", m, ti)])
                        continue
                    P.add("dve", lambda h, m=m, b=b, t0=t0, tn=tn, i=i: h.scalar_tensor_tensor(
                        out=hT[:, m, t0:t0 + tn], in0=ps[:, b, 0:tn], scalar=der[:, par, i, 2, m:m + 1],
                        in1=hT[:, m, t0:t0 + tn], op0=ALU.mult, op1=ALU.add),
                        reads=[("der", par, i, 0, 2)], writes=[pk(b), ("hT", m, ti)])

        def acol(t):
            return 1 + t if t < TL else 2051 + (t - TL)

        def pcol(t):
            return 8 + t if t < TL else 2072 + (t - TL)

        def mixer_phase(l, par):
            wl = w_in[l].rearrange("(k p) n -> p k n", p=128)
            s0 = alloc_slot()
            load_slot(wl[:, :, 0:512], s0, (8, 512))
            def qk_iter(it, c, ti):
                gcol = 0 if c < 2 else 1
                t0, tn = TILES[ti]
                u = it % 2
                base = (S0, S1)[u]
                A0 = base[:, 0:512]
                A1 = base[:, 512:1024]
                A2 = base[:, 1024:1536]
                A3 = base[:, 1536:2048]
                QB = base[:, 2048:2304].bitcast(BF16)
                qk = lambda i: ("QK", u, i)
                st = {}

                def part0():
                    st["b"] = dense_bank()
                    mm8(st["b"], s0, c, ti)

                def part1():
                    b = st["b"]
                    P.add("act", lambda h: h.activation(
                        out=A0[:, 0:tn], in_=ps[:, b, 0:tn], func=AF.Identity, scale=sm[:, gcol:gcol + 1]),
                        reads=["sm01"], writes=[pk(b), qk(0)])
                    if ti < 4:
                        P.add("dve", lambda h: h.tensor_scalar(
                            out=QB[:, 0:tn], in0=ps[:, b, 0:tn], scalar1=sm[:, gcol:gcol + 1], scalar2=None, op0=ALU.mult),
                            reads=["sm01"], writes=[pk(b), qk(4)])
                    P.add("act", lambda h: h.activation(out=SQ[u][:, 0:tn], in_=ps[:, b, 0:tn], func=AF.Square),
                          reads=[], writes=[pk(b), ("SQ", u)])
                    mb = misc_bank()
                    st["mb"] = mb
                    P.add("pe", lambda h: h.matmul(ps[:, mb, 0:tn], lhsT=bones, rhs=SQ[u][:, 0:tn], start=True, stop=True),
                          reads=[("SQ", u), "cst"], writes=[pk(mb)])
                    if ti < 4:
                        mb2 = tok_bank()
                        st["mb2"] = mb2
                        P.add("pe", lambda h: h.matmul(ps[:, mb2, 0:tn], lhsT=Rm, rhs=QB[:, 0:tn], start=True, stop=True),
                              reads=[qk(4), "cst"], writes=[pk(mb2)])

                def part2():
                    mb = st["mb"]
                    P.add("act", lambda h: h.activation(
                        out=A1[:, 0:tn], in_=ps[:, mb, 0:tn], func=AF.Ln, bias=epsc[:, 1:2], scale=1.0),
                        reads=["epsc"], writes=[pk(mb), qk(1)])
                    P.add("act", lambda h: h.activation(
                        out=A1[:, 0:tn], in_=A1[:, 0:tn], func=AF.Exp, scale=-0.5),
                        reads=[qk(1)], writes=[qk(1)])
                    if c < 2:
                        dst = qT[:, c, t0:t0 + tn]
                        dkey = ("qT", c, ti)
                    else:
                        dst = kT[:, t0:t0 + tn]
                        dkey = ("kT", ti)
                    if ti < 4:
                        mb2 = st["mb2"]
                        P.add("pool", lambda h: h.tensor_tensor(
                            out=A2[:, 0:tn], in0=A0[:, 0:tn], in1=cosT[:, t0:t0 + tn], op=ALU.mult),
                            reads=[qk(0), "cos"], writes=[qk(2)])
                        P.add("dve", lambda h: h.tensor_tensor(
                            out=A3[:, 0:tn], in0=ps[:, mb2, 0:tn], in1=sinT[:, t0:t0 + tn], op=ALU.mult),
                            reads=["sin"], writes=[pk(mb2), qk(3)])
                        P.add("pool", lambda h: h.tensor_tensor(
                            out=A2[:, 0:tn], in0=A2[:, 0:tn], in1=A3[:, 0:tn], op=ALU.add),
                            reads=[qk(2), qk(3)], writes=[qk(2)])
                        P.add("dve", lambda h: h.tensor_tensor(
                            out=dst, in0=A2[:, 0:tn], in1=A1[:, 0:tn], op=ALU.mult),
                            reads=[qk(2), qk(1)], writes=[dkey])
                    else:
                        P.add("dve", lambda h: h.tensor_tensor(
                            out=dst, in0=A0[:, 0:tn], in1=A1[:, 0:tn], op=ALU.mult),
                            reads=[qk(0), qk(1)], writes=[dkey])
                return part0, part1, part2

            iters = []
            for c in range(3):
                for ti in range(5):
                    iters.append(qk_iter(len(iters), c, ti))
            NI = len(iters)
            iters[0][0]()
            iters[1][0]()
            iters[0][1]()
            for ii in range(NI):
                if ii + 2 < NI:
                    iters[ii + 2][0]()
                if ii + 1 < NI:
                    iters[ii + 1][1]()
                iters[ii][2]()
            for blk in range(18):
                tb = tok_bank()
                ti = min(blk // 4, 4)
                for k in range(8):
                    P.add("pe", lambda h, k=k, blk=blk, tb=tb: h.matmul(
                        ps[:, tb, 0:128], lhsT=aT[:, k, blk * 128:(blk + 1) * 128],
                        rhs=wr[:, s0, k * 512 + 384:k * 512 + 512], start=(k == 0), stop=(k == 7)),
                        reads=[("wr", s0), ("aT", k, ti)], writes=[pk(tb)])
                P.add("act", lambda h, blk=blk, tb=tb: h.activation(out=Vt[:, blk, :], in_=ps[:, tb, 0:128], func=AF.Copy),
                      reads=[], writes=[pk(tb), ("V", blk)])

            release_slot(s0)
            so0 = alloc_slot()
            load_slot(w_out[l, 0:512, :].rearrange("(a p) n -> p a n", p=128), so0, (4, 1024))

            PT2 = [ph[:, PTo + i * 256:PTo + (i + 1) * 256].bitcast(BF16) for i in range(2)]
            PT2 += [TP[3][:, i * 256:(i + 1) * 256].bitcast(BF16) for i in range(2)]
            QX = [TP[2][:, i * 256:(i + 1) * 256].bitcast(BF16) for i in range(2)]
            for i in range(2):
                P.add("pool", lambda h, i=i: h.memset(QX[i][:, 0:512], 0.0),
                      writes=[("QX", i)] + [("TP", x) for x in range(5)])
            steps = []
            for n in range(18):
                if n < 16:
                    kcs = []
                    if n > 0:
                        kcs.append((n - 1, mprev))
                    kcs.append((n, None))
                    if n < 15:
                        kcs.append((n + 1, mnext))
                    kcs += [(16, None), (17, None)]
                else:
                    kcs = [(16, None), (17, None)]
                for ci, (kc, mask) in enumerate(kcs):
                    steps.append(dict(n=n, ci=ci, kc=kc, mask=mask, last=(ci == len(kcs) - 1)))

            def emit_qx(n):
                nti = min(n // 4, 4)
                qx = QX[n % 2]
                P.add("pool", lambda h: h.tensor_copy(
                    out=qx[0:64, 0:256].rearrange("p (c t) -> p c t", c=2), in_=qT[0:64, :, n * 128:(n + 1) * 128]),
                    reads=[("qT", 0, nti), ("qT", 1, nti)], writes=[("QX", n % 2)])
                P.add("pool", lambda h: h.tensor_copy(
                    out=qx[64:128, 256:512].rearrange("p (c t) -> p c t", c=2), in_=qT[64:128, :, n * 128:(n + 1) * 128]),
                    reads=[("qT", 0, nti), ("qT", 1, nti)], writes=[("QX", n % 2)])

            def emit_S(st):
                n, kc, mask = st["n"], st["kc"], st["mask"]
                kti = min(kc // 4, 4)
                sbk = (4, 5, 2, 3)[st["idx"] % 4]
                st["sbk"] = sbk
                st["pti"] = st["idx"] % 4
                qx = QX[n % 2]
                P.add("pe", lambda h: h.matmul(
                    ps[:, sbk, 0:512], lhsT=kT[:, kc * 128:(kc + 1) * 128], rhs=qx[:, 0:512],
                    start=True, stop=(mask is None)),
                    reads=[("kT", kti), ("QX", n % 2)], writes=[pk(sbk)])
                if mask is not None:
                    P.add("pe", lambda h: h.matmul(
                        ps[:, sbk, 0:512].rearrange("p (a t) -> p a t", a=2), lhsT=ident,
                        rhs=mask.unsqueeze(1).broadcast_to([128, 2, 256]), start=False, stop=True),
                        reads=["cst"], writes=[pk(sbk)])

            def emit_rest(st):
                n, kc, ci, last = st["n"], st["kc"], st["ci"], st["last"]
                sbk = st["sbk"]
                pt = PT2[st["pti"]]
                ptk = ("PT", st["pti"])
                ob, db = (6, 7) if n % 2 == 0 else (0, 1)
                P.add("pe", lambda h: h.matmul(
                    ps[:, ob, 0:512], lhsT=Vt[:, kc, :], rhs=pt[:, 0:512], start=(ci == 0), stop=last),
                    reads=[("V", kc), ptk], writes=[pk(ob)])
                P.add("pe", lambda h: h.matmul(
                    ps[:, db, 0:512], lhsT=ones, rhs=pt[:, 0:512], start=(ci == 0), stop=last),
                    reads=["cst", ptk], writes=[pk(db)])
                if last:
                    fin_q.append((st, 0))
                    if dbg.get("nodefer"):
                        while fin_q:
                            pump_fin()

            fin_q = []

            def emit_fin(st, stage):
                n = st["n"]
                ob, db = (6, 7) if n % 2 == 0 else (0, 1)
                nti = min(n // 4, 4)
                tf = TP[n % 2]
                tfk = ("TP", n % 2)
                if stage == 0:
                    P.add("dve", lambda h: h.tensor_tensor(
                        out=tf[:, 0:512].rearrange("p (c t) -> p c t", c=4),
                        in0=ps[:, db, 0:512].rearrange("p (c t) -> p c t", c=4),
                        in1=sm[:, 12:16].unsqueeze(2).broadcast_to([128, 4, 128]), op=ALU.add),
                        reads=["sm12"], writes=[pk(db), tfk])
                elif stage == 1:
                    P.add("act", lambda h: h.activation(out=tf[:, 0:512], in_=tf[:, 0:512], func=AF.Ln),
                          reads=[tfk], writes=[tfk])
                    P.add("act", lambda h: h.activation(out=tf[:, 0:512], in_=tf[:, 0:512], func=AF.Exp, scale=-1.0),
                          reads=[tfk], writes=[tfk])
                else:
                    P.add("dve", lambda h: h.tensor_tensor(
                        out=Y[0:64, :, n * 128:(n + 1) * 128],
                        in0=ps[0:64, ob, 0:256].rearrange("p (c t) -> p c t", c=2),
                        in1=tf[0:64, 0:256].rearrange("p (c t) -> p c t", c=2), op=ALU.mult),
                        reads=[tfk], writes=[pk(ob), ("Y", 0, nti), ("Y", 1, nti)])
                    P.add("dve", lambda h: h.tensor_tensor(
                        out=Y[64:128, :, n * 128:(n + 1) * 128],
                        in0=ps[64:128, ob, 256:512].rearrange("p (c t) -> p c t", c=2),
                        in1=tf[64:128, 256:512].rearrange("p (c t) -> p c t", c=2), op=ALU.mult),
                        reads=[tfk], writes=[pk(ob), ("Y", 0, nti), ("Y", 1, nti)])

            def emit_exp(st):
                sbk = st["sbk"]
                pt = PT2[st["pti"]]
                P.add("act", lambda h: h.activation(
                    out=pt[:, 0:512], in_=ps[:, sbk, 0:512], func=AF.Exp, scale=0.125),
                    reads=[], writes=[pk(sbk), ("PT", st["pti"])])

            for si, st in enumerate(steps):
                st["idx"] = si
            emit_qx(0)
            emit_qx(1)
            emit_S(steps[0])
            emit_S(steps[1])
            def pump_fin():
                nq = []
                for (fst, stage) in fin_q:
                    emit_fin(fst, stage)
                    if stage < 2:
                        nq.append((fst, stage + 1))
                fin_q[:] = nq

            for si, st in enumerate(steps):
                emit_exp(st)
                pump_fin()
                if si + 2 < len(steps):
                    emit_S(steps[si + 2])
                emit_rest(st)
                if st["last"] and st["n"] + 2 < 18:
                    emit_qx(st["n"] + 2)
            while fin_q:
                pump_fin()
            if 'B' in MIX:
                wout_partial(par, so0, 0)

            s1 = alloc_slot()
            load_slot(wl[:, :, 512:1024], s1, (8, 512))
            s2 = alloc_slot()
            load_slot(wl[:, :, 1024:1536], s2, (8, 512))
            amap = {0: ((s1, 0), (s1, 1), (s1, 2)), 1: ((s1, 3), (s2, 0), (s2, 1))}
            for j in range(2):
                (sh, ch), (sc, cc), (sg, cg) = amap[j]
                for ti, (t0, tn) in enumerate(TILES):
                    b = dense_bank()
                    mm8(b, sh, ch, ti)
                    a0 = acol(t0)
                    P.add("act", lambda h, b=b, a0=a0, tn=tn: h.activation(out=S0[:, a0:a0 + tn], in_=ps[:, b, 0:tn], func=AF.Copy),
                          reads=[], writes=[pk(b), ("S0", ti)])
                for (c0, c1) in ((0, 1), (2049, 2051), (2307, 2308)):
                    P.add("pool", lambda h, c0=c0, c1=c1: h.memset(S1[:, c0:c1], 0.0), writes=[("S1", "pad")])
                for ti, (t0, tn) in enumerate(TILES):
                    b = dense_bank()
                    mm8(b, sc, cc, ti)
                    a0 = acol(t0)
                    P.add("dve", lambda h, b=b, a0=a0, tn=tn: h.tensor_tensor(
                        out=S1[:, a0:a0 + tn], in0=ps[:, b, 0:tn], in1=S0[:, a0:a0 + tn], op=ALU.mult),
                        reads=[("S0", ti)], writes=[pk(b), ("S1", ti)])
                allS0 = [("S0", ti) for ti in range(5)]
                allS1 = [("S1", ti) for ti in range(5)] + [("S1", "pad")]
                cw = 6 + j * 3
                P.add("act", lambda h, cw=cw: h.activation(
                    out=S0[:, 1:2307], in_=S1[:, 1:2307], func=AF.Identity, scale=sm[:, cw + 1:cw + 2]),
                    reads=allS1 + ["smconv"], writes=allS0)
                P.add("dve", lambda h, cw=cw: h.scalar_tensor_tensor(
                    out=S0[:, 1:2307], in0=S1[:, 0:2306], scalar=sm[:, cw:cw + 1], in1=S0[:, 1:2307],
                    op0=ALU.mult, op1=ALU.add), reads=allS1 + ["smconv"], writes=allS0)
                P.add("dve", lambda h, cw=cw: h.scalar_tensor_tensor(
                    out=S0[:, 1:2307], in0=S1[:, 2:2308], scalar=sm[:, cw + 2:cw + 3], in1=S0[:, 1:2307],
                    op0=ALU.mult, op1=ALU.add), reads=allS1 + ["smconv"], writes=allS0)
                for ti, (t0, tn) in enumerate(TILES):
                    b = dense_bank()
                    mm8(b, sg, cg, ti)
                    a0 = acol(t0)
                    P.add("dve", lambda h, b=b, a0=a0, t0=t0, tn=tn, j=j: h.tensor_tensor(
                        out=Y[:, j, t0:t0 + tn], in0=ps[:, b, 0:tn], in1=S0[:, a0:a0 + tn], op=ALU.mult),
                        reads=[("S0", ti)], writes=[pk(b), ("Y", j, ti)])
            if 'A' in MIX:
                wout_partial(par, so0, 1)

            release_slot(so0)
            release_slot(s1)
            so1 = alloc_slot()
            load_slot(w_out[l, 512:1024, :].rearrange("(a p) n -> p a n", p=128), so1, (4, 1024))
            SU = (S0, S1)
            for j in range(2):
                for ti, (t0, tn) in enumerate(TILES):
                    b = dense_bank()
                    mm8(b, s2, 2 + j, ti)
                    P.add("act", lambda h, b=b, t0=t0, tn=tn, j=j: h.activation(
                        out=SU[j][:, t0:t0 + tn], in_=ps[:, b, 0:tn], func=AF.Copy),
                        reads=[], writes=[pk(b), ("S%d" % j, ti)])
            release_slot(s2)
            s3 = alloc_slot()
            load_slot(wl[:, :, 1536:2048], s3, (8, 512))
            def c_bufs(blk):
                o = R1o + (blk % 3) * 768
                return (ph[:, o:o + 256], ph[:, o + 256:o + 512], ph[:, o + 512:o + 640].bitcast(BF16), (blk % 3))

            c_tb = {}

            def c_stage0(blk):
                tb = tok_bank()
                c_tb[blk] = tb
                ti = min(blk // 4, 4)
                for k in range(8):
                    P.add("pe", lambda h, k=k: h.matmul(
                        ps[:, tb, 0:256], lhsT=aT[:, k, blk * 128:(blk + 1) * 128],
                        rhs=wr[:, s3, k * 512:k * 512 + 256], start=(k == 0), stop=(k == 7)),
                        reads=[("wr", s3), ("aT", k, ti)], writes=[pk(tb)])

            def c_stage1(blk):
                tb = c_tb[blk]
                ti = min(blk // 4, 4)
                raw, sqb, vn, si = c_bufs(blk)
                so = 4 * si
                P.add("act", lambda h: h.activation(out=raw[:, 0:256], in_=ps[:, tb, 0:256], func=AF.Copy),
                      reads=[], writes=[pk(tb), ("Craw", si)])
                P.add("pool", lambda h: h.tensor_tensor(out=sqb[:, 0:256], in0=raw[:, 0:256], in1=raw[:, 0:256], op=ALU.mult),
                      reads=[("Craw", si)], writes=[("Csq", si)])
                P.add("dve", lambda h: h.reduce_sum(out=ssq[:, so:so + 1], in_=sqb[:, 0:256], axis=AX.X),
                      reads=[("Csq", si)], writes=[("ssq", so)])
                P.add("act", lambda h: h.activation(out=ssq[:, so + 2:so + 3], in_=ssq[:, so:so + 1], func=AF.Sqrt,
                                                    bias=epsc[:, 2:3], scale=1.0),
                      reads=[("ssq", so), "epsc"], writes=[("ssq", so + 2)])
                P.add("dve", lambda h: h.reciprocal(out=ssq[:, so + 1:so + 2], in_=ssq[:, so + 2:so + 3]),
                      reads=[("ssq", so + 2)], writes=[("ssq", so + 1)])
                P.add("dve", lambda h: h.scalar_tensor_tensor(
                    out=vn[:, 0:256], in0=raw[:, 0:256], scalar=ssq[:, so + 1:so + 2], in1=sgn[:], op0=ALU.mult, op1=ALU.mult),
                    reads=[("Craw", si), ("ssq", so + 1), "sgn"], writes=[("Cvn", si)])
            def c_stage2(blk):
                ti = min(blk // 4, 4)
                raw, sqb, vn, si = c_bufs(blk)
                for j in range(2):
                    zb = misc_bank()
                    for gg in range(2):
                        g = 2 * j + gg
                        P.add("pe", lambda h, zb=zb, gg=gg, g=g, vn=vn: h.matmul(
                            ps[gg * 64:(gg + 1) * 64, zb, 0:128], lhsT=vn[:, g * 64:(g + 1) * 64], rhs=wst[:, g, :],
                            start=True, stop=True), reads=[("Cvn", si), "wst"], writes=[pk(zb)])
                    P.add("dve", lambda h, zb=zb, j=j: h.tensor_tensor(
                        out=TP[4][:, j * 128:(j + 1) * 128], in0=ps[:, zb, 0:128], in1=bst[:, j, :], op=ALU.add),
                        reads=["bst"], writes=[pk(zb), ("TP4", j)])
                    P.add("pool", lambda h, j=j, blk=blk: h.tensor_tensor(
                        out=Y[:, j, blk * 128:(blk + 1) * 128], in0=TP[4][:, j * 128:(j + 1) * 128],
                        in1=SU[j][:, blk * 128:(blk + 1) * 128], op=ALU.mult),
                        reads=[("TP4", j), ("S%d" % j, ti)], writes=[("Y", j, ti)])
            c_stage0(0)
            c_stage1(0)
            c_stage0(1)
            c_stage1(1)
            for blk in range(18):
                if blk + 2 < 18:
                    c_stage0(blk + 2)
                c_stage2(blk)
                if blk + 2 < 18:
                    c_stage1(blk + 2)
            if 'C' in MIX:
                wout_partial(par, so1, 0)

            allS = lambda nm: [(nm, ti) for ti in range(5)] + [(nm, "pad")]
            for j in range(2):
                for ti, (t0, tn) in enumerate(TILES):
                    b = dense_bank()
                    mm8(b, s3, 2 + j, ti)
                    p0 = pcol(t0)
                    P.add("act", lambda h, b=b, p0=p0, tn=tn: h.activation(out=S0[:, p0:p0 + tn], in_=ps[:, b, 0:tn], func=AF.Copy),
                          reads=[], writes=[pk(b), ("S0", ti)])
                for (c0, c1) in ((0, 8), (2056, 2072), (2328, 2336)):
                    P.add("pool", lambda h, c0=c0, c1=c1: h.memset(S0[:, c0:c1], 0.0), writes=[("S0", "pad")])
                P.add("dve", lambda h: h.tensor_tensor(out=S1[:, 1:2335], in0=S0[:, 0:2334], in1=S0[:, 1:2335], op=ALU.add),
                      reads=allS("S0"), writes=allS("S1"))
                if j == 0:
                    P.add("dve", lambda h: h.tensor_tensor(out=S2[64:128, 2:2334], in0=S1[64:128, 1:2333],
                                                            in1=S1[64:128, 3:2335], op=ALU.add),
                          reads=allS("S1"), writes=["S2hi"])
                    P.add("dve", lambda h: h.tensor_copy(out=S2[0:64, 8:2328], in_=S1[0:64, 8:2328]),
                          reads=allS("S1"), writes=["S2lo"])
                else:
                    P.add("dve", lambda h: h.tensor_tensor(out=S2[:, 2:2334], in0=S1[:, 1:2333], in1=S1[:, 3:2335],
                                                            op=ALU.add), reads=allS("S1"), writes=["S2hi", "S2lo"])
                    P.add("dve", lambda h: h.tensor_tensor(out=S1[:, 4:2332], in0=S2[:, 2:2330], in1=S2[:, 6:2334],
                                                           op=ALU.add), reads=["S2hi", "S2lo"], writes=allS("S1"))
                    P.add("dve", lambda h: h.tensor_tensor(out=S2[64:128, 8:2328], in0=S1[64:128, 4:2324],
                                                            in1=S1[64:128, 12:2332], op=ALU.add),
                          reads=allS("S1"), writes=["S2hi"])
                    P.add("dve", lambda h: h.tensor_copy(out=S2[0:64, 8:2328], in_=S1[0:64, 8:2328]),
                          reads=allS("S1"), writes=["S2lo"])
                P.add("dve", lambda h, j=j: h.scalar_tensor_tensor(
                    out=dT[:, 8:2328], in0=S2[:, 8:2328], scalar=rce[:, 32 + j:33 + j], in1=S0[:, 8:2328],
                    op0=ALU.mult, op1=ALU.subtract), reads=["S2hi", "S2lo", "rce"] + allS("S0"), writes=["dT"])
                for (base, tlen) in ((8, TL), (2072, TC)):
                    for side in range(2):
                        c0 = base if side == 0 else base + tlen - 8
                        ro = j * 16 + side * 8
                        P.add("pool", lambda h, c0=c0, ro=ro: h.tensor_tensor(
                            out=TP[0][:, 0:8], in0=S2[:, c0:c0 + 8], in1=rce[:, ro:ro + 8], op=ALU.mult),
                            reads=["S2hi", "S2lo", "rce"], writes=[("TP", 0)])
                        P.add("pool", lambda h, c0=c0: h.tensor_tensor(
                            out=dT[:, c0:c0 + 8], in0=TP[0][:, 0:8], in1=S0[:, c0:c0 + 8], op=ALU.subtract),
                            reads=[("TP", 0)] + allS("S0"), writes=["dT"])
                for ti, (t0, tn) in enumerate(TILES):
                    mb = misc_bank()
                    p0 = pcol(t0)
                    P.add("pe", lambda h, mb=mb, p0=p0, tn=tn, j=j: h.matmul(
                        ps[:, mb, 0:tn], lhsT=wpb[:, j, :], rhs=dT[:, p0:p0 + tn], start=True, stop=True),
                        reads=["dT", "wpb"], writes=[pk(mb)])
                    P.add("act", lambda h, mb=mb, t0=t0, tn=tn, j=j: h.activation(
                        out=Y[:, j, t0:t0 + tn], in_=ps[:, mb, 0:tn], func=AF.Identity, scale=pp[:, 68 + j:69 + j]),
                        reads=["pp"], writes=[pk(mb), ("Y", j, ti)])
            if 'D' in MIX:
                wout_partial(par, so1, 1, tile_outer=True)
            release_slot(s3)
            release_slot(so1)

        def ffn_phase(l, par, nxt):
            if nxt is not None:
                load_params(nxt)
            w1v = w_ff1[l].rearrange("(k p) n -> p k n", p=128)

            def ld1(g):
                sl = alloc_slot()
                load_slot(w1v[:, :, g * 512:(g + 1) * 512], sl, (8, 512))
                return sl

            def ld2(g):
                sl = alloc_slot()
                load_slot(w_ff2[l, g * 512:(g + 1) * 512, :].rearrange("(a p) n -> p a n", p=128), sl, (4, 1024))
                return sl

            mods = list(range(12)) if nxt is not None else []

            def next_mod():
                if not mods:
                    return None
                cg = mods.pop(0)
                return (mod_load(nxt, cg), cg)

            sa = ld1(0)
            sb_ = ld2(0)
            ma = next_mod()
            mod_done = [nxt is None]
            norm_phase(par, 1, cfg["nt"])
            for g in range(8):
                for jj in range(4):
                    for ti, (t0, tn) in enumerate(TILES[:cfg["nt"]]):
                        b = dense_bank()
                        mm8(b, sa, jj, ti)
                        rt = cnt["dense"] % 2
                        P.add("act", lambda h, b=b, tn=tn, rt=rt: h.activation(out=RT[rt][:, 0:tn], in_=ps[:, b, 0:tn], func=AF.Relu),
                              reads=[], writes=[pk(b), ("TP", 3 + rt)])
                        P.add("pool", lambda h, rt=rt, jj=jj, t0=t0, tn=tn: h.tensor_tensor(
                            out=gT[:, jj, t0:t0 + tn], in0=RT[rt][:, 0:tn], in1=RT[rt][:, 0:tn], op=ALU.mult),
                            reads=[("TP", 3 + rt)], writes=[("gT", jj, ti)])
                release_slot(sa)
                mb_ = next_mod()
                if ma is not None:
                    mod_mm(*ma)
                sa_n = ld1(g + 1) if g + 1 < 8 else None
                order = [(m, ti) for m in range(8) for ti in range(cfg["nt"])]
                if g == 7:
                    order = [(m, ti) for ti in range(cfg["nt"]) for m in range(8)]
                for (m, ti) in order:
                    for (t0, tn) in (TILES[ti],):
                        i = tile_i(ti)
                        b = dense_bank()
                        for jj in range(4):
                            off = jj * 1024 + m * 128
                            P.add("pe", lambda h, jj=jj, off=off, b=b, t0=t0, tn=tn, sb_=sb_: h.matmul(
                                ps[:, b, 0:tn], lhsT=wr[:, sb_, off:off + 128], rhs=gT[:, jj, t0:t0 + tn],
                                start=(jj == 0), stop=(jj == 3)),
                                reads=[("wr", sb_), ("gT", jj, ti)], writes=[pk(b)])
                        P.add("dve", lambda h, m=m, b=b, t0=t0, tn=tn, i=i: h.scalar_tensor_tensor(
                            out=hT[:, m, t0:t0 + tn], in0=ps[:, b, 0:tn], scalar=der[:, par, i, 5, m:m + 1],
                            in1=hT[:, m, t0:t0 + tn], op0=ALU.mult, op1=ALU.add),
                            reads=[("der", par, i, 1, 2)], writes=[pk(b), ("hT", m, ti)])
                release_slot(sb_)
                sb_n = ld2(g + 1) if g + 1 < 8 else None
                if mb_ is not None:
                    mod_mm(*mb_)
                ma = next_mod()
                if not mod_done[0] and not mods and ma is None:
                    mod_finish(1 - par)
                    mod_done[0] = True
                sa, sb_ = sa_n, sb_n

        load_params(0)
        pend = [(mod_load(0, 0), 0), (mod_load(0, 1), 1)]
        for cg in range(12):
            mod_mm(*pend.pop(0))
            if cg + 2 < 12:
                pend.append((mod_load(0, cg + 2), cg + 2))
        mod_finish(0)
        for l in range(L):
            par = l % 2
            if l > 0 and not DOFFN:
                load_params(l)
                mod_finish(par)
            layer_small(l)
            P.add("dve", lambda h: h.tensor_copy(out=sm[:, 6:12], in_=pp[:, 70:76]), reads=["pp"], writes=["smconv"])
            norm_phase(par, 0)
            cfg["nt"] = 4 if (l == L - 1 and L == DEPTH) else 5
            mixer_phase(l, par)
            if DOFFN:
                ffn_phase(l, par, l + 1 if l + 1 < L else None)
        outs = []
        for k in range(8):
            outs.append(P.add("sp", lambda h, k=k: h.dma_start(out=outT[k * 128:(k + 1) * 128, :], in_=hT[:, k, :]),
                              reads=[("hT", k, ti) for ti in range(5)], dma=True))
        for o in outs:
            P.final.append(("sp", o))

        with nc.Block() as block:
            handles = {}

            @block.tensor
            def _(h):
                P_emit_one(P, "pe", h, esem, dsems)

            @block.scalar
            def _(h):
                P_emit_one(P, "act", h, esem, dsems)

            @block.vector
            def _(h):
                P_emit_one(P, "dve", h, esem, dsems)

            @block.gpsimd
            def _(h):
                P_emit_one(P, "pool", h, esem, dsems)

            @block.sync
            def _(h):
                P_emit_one(P, "sp", h, esem, dsems)
    return nc


def P_emit_one(P, e, h, esem, dsems):
    if not getattr(P, "_ranked", False):
        for ee in ENGS:
            r = 0
            for op in P.ops[ee]:
                if not op.dma and op.marked:
                    r += 1
                    op.rank = r
        P._ranked = True
    known = {}
    for op in P.ops[e]:
        need = {}
        for d in op.deps:
            if d.dma:
                key = ("d", d.dsem)
                val = d.dval
            else:
                key = ("e", d.eng)
                val = d.rank
            if need.get(key, 0) < val:
                need[key] = val
        if op.dma and op.prev_dval > 0:
            key = ("d", op.dsem)
            if need.get(key, 0) < op.prev_dval:
                need[key] = op.prev_dval
        for key, val in need.items():
            if known.get(key, 0) >= val:
                continue
            known[key] = val
            sem = dsems[key[1]] if key[0] == "d" else esem[key[1]]
            h.wait_ge(sem, val)
        ins = op.fn(h)
        if op.dma:
            ins.then_inc(dsems[op.dsem], 16)
        elif op.marked:
            ins.then_inc(esem[e], 1)
    for (fe, dop) in P.final:
        if fe == e:
            h.wait_ge(dsems[dop.dsem], dop.dval)


def _consts():
    rows = np.repeat(np.arange(TL // 64), 64).astype(np.float32)
    cols = np.tile(np.arange(64), TL // 64).astype(np.float32)
    inv = (10000.0 ** (-np.arange(16, dtype=np.float32) / 16)).astype(np.float32)
    cosd = np.zeros((128, TL), np.float32)
    sind = np.zeros((128, TL), np.float32)
    for p in range(128):
        d = p % 64
        pos = rows if d < 32 else cols
        ang = pos * inv[d % 16]
        cosd[p] = np.cos(ang)
        sgn = -1.0 if (d % 32) < 16 else 1.0
        sind[p] = sgn * np.sin(ang)
    cst = np.zeros((128, 1024), np.float32)
    cst[:, 0:128] = 1.0
    cst[0:64, 128:192] = 1.0
    cst[64:128, 192:256] = 1.0
    for m in range(128):
        k = m + 16 if (m % 32) < 16 else m - 16
        cst[k, 256 + m] = 1.0
    cst[:, 384:512] = np.eye(128, dtype=np.float32)
    kk = np.arange(128)[:, None]
    qq = np.arange(128)[None, :]
    mp = np.where(kk >= qq, 0.0, -30000.0).astype(np.float32)
    mn = np.where(kk <= qq, 0.0, -30000.0).astype(np.float32)
    cst[:, 512:640] = mp
    cst[:, 640:768] = mp
    cst[:, 768:896] = mn
    cst[:, 896:1024] = mn
    rce = np.zeros((128, 36), np.float32)
    for j in range(2):
        for p in range(128):
            w = ((2, 4), (8, 16))[j][p // 64]
            rce[p, 32 + j] = 1.0 / w
            for side in range(2):
                for e in range(8):
                    if side == 0:
                        t = e
                        cntv = min(t + w // 2, 10 ** 6) - max(t - w // 2, 0)
                    else:
                        dist = 8 - e
                        cntv = min(w // 2, dist) + w // 2
                    rce[p, j * 16 + side * 8 + e] = 1.0 / cntv
    return cosd, sind, cst, rce


def _prep(inputs, L):
    f = lambda a: np.ascontiguousarray(np.asarray(a, dtype=np.float32))
    x = f(inputs["x"]); ctx = f(inputs["ctx"]); c = f(inputs["c"]); c_ctx = f(inputs["c_ctx"])
    w_in = f(inputs["w_in"])[:L]; w_out = f(inputs["w_out"])[:L]
    A0, B0, KV0, C0, D0 = 0, 768, 1024, 1280, 1792
    qa = np.r_[B0 + 0:B0 + 64, B0 + 128:B0 + 192]
    qb = np.r_[B0 + 64:B0 + 128, B0 + 192:B0 + 256]
    kcols = np.r_[KV0:KV0 + 128]
    vcols = np.r_[KV0 + 128:KV0 + 256]
    h0, h1 = np.r_[0:128], np.r_[128:256]
    gb0, gb1 = np.r_[256:384], np.r_[384:512]
    gc0, gc1 = np.r_[512:640], np.r_[640:768]
    u0, u1 = np.r_[C0:C0 + 128], np.r_[C0 + 128:C0 + 256]
    vs = np.r_[C0 + 256:C0 + 512]
    x0, x1 = np.r_[D0:D0 + 128], np.r_[D0 + 128:D0 + 256]
    perm = np.concatenate([qa, qb, kcols, vcols, h0, gc0, gb0, h1, gc1, gb1, u0, u1, vs, x0, x1])
    assert perm.shape[0] == 2048 and len(set(perm.tolist())) == 2048
    w_in_p = np.ascontiguousarray(w_in[:, :, perm])
    ya = np.r_[0:256]
    yb = np.r_[256 + 0:256 + 64, 256 + 128:256 + 192, 256 + 64:256 + 128, 256 + 192:256 + 256]
    yc = np.r_[512:768]
    yd = np.r_[768:1024]
    rperm = np.concatenate([yb, ya, yc, yd])
    w_out_p = np.ascontiguousarray(w_out[:, rperm, :])
    pp = np.zeros((L, 128, NPP), np.float32)
    fm = lambda v, n: v.reshape(n, 128).T
    for l in range(L):
        pp[l, :, 0:48] = fm(f(inputs["b_ada"])[l], 48)
        pp[l, :, 48:56] = fm(f(inputs["norm_mix"])[l], 8)
        pp[l, :, 56:64] = fm(f(inputs["norm_ff"])[l], 8)
        pp[l, :, 64] = np.tile(f(inputs["q_norm"])[l], 2)
        pp[l, :, 65] = np.tile(f(inputs["k_norm"])[l], 2)
        sk = f(inputs["sink"])[l]
        pp[l, 0:64, 66] = sk[0]; pp[l, 64:128, 66] = sk[2]
        pp[l, 0:64, 67] = sk[1]; pp[l, 64:128, 67] = sk[3]
        pp[l, :, 76:80] = sk[None, 0:4]
        pp[l, :, 68:70] = fm(f(inputs["pool_scale"])[l], 2)
        cw = f(inputs["conv_w"])[l]
        for j in range(2):
            for tap in range(3):
                pp[l, :, 70 + j * 3 + tap] = cw[tap, j * 128:(j + 1) * 128]
    sgn = np.ascontiguousarray(np.broadcast_to(f(inputs["sgu_norm"])[:L, None, :], (L, 128, 256)))
    bs = f(inputs["b_sgu"])[:L]
    bst = np.zeros((L, 128, 2, 128), np.float32)
    for j in range(2):
        bst[:, 0:64, j, :] = bs[:, 2 * j, None, :]
        bst[:, 64:128, j, :] = bs[:, 2 * j + 1, None, :]
    bst = bst.reshape(L, 128, 256)
    ws = f(inputs["w_sgu"])[:L]
    wst = np.ascontiguousarray(ws.transpose(0, 3, 1, 2)).reshape(L, 128, 512)
    cosd, sind, cst, rce = _consts()
    shared = {
        "w_ada": f(inputs["w_ada"])[:L], "w_in": w_in_p, "w_out": w_out_p,
        "w_ff1": f(inputs["w_ff1"])[:L], "w_ff2": f(inputs["w_ff2"])[:L],
        "pp": pp, "sgn": sgn, "bst": bst, "wst": wst, "wpool": f(inputs["w_pool"])[:L],
        "cosd": cosd, "sind": sind, "cstd": cst, "rced": rce,
    }
    in_maps = []
    for b in range(8):
        m = dict(shared)
        m["hT0"] = np.ascontiguousarray(np.concatenate([x[b].T, ctx[b].T], axis=1))
        cv = np.zeros((128, 8, 2), np.float32)
        cv[:, :, 0] = c[b].reshape(8, 128).T
        cv[:, :, 1] = c_ctx.reshape(8, 128).T
        m["cvec"] = cv.reshape(128, 16)
        in_maps.append(m)
    return in_maps


def run(inputs, L=DEPTH, trace=False, dbg=None):
    nc = build(L, dbg)
    in_maps = _prep(inputs, L)
    res = run_bass_kernel_spmd(nc, in_maps, core_ids=list(range(8)), trace=trace)
    full = np.stack([np.ascontiguousarray(r["outT"].T) for r in res.results], axis=0).astype(np.float32)
    if dbg is not None:
        return full, res
    return np.ascontiguousarray(full[:, :TL, :]), res


def kernel(**inputs):
    out, _ = run(inputs, DEPTH)
    return out
```

```python
import numpy as np
import concourse.bass as bass
import concourse.mybir as mybir
from concourse.bass_utils import run_bass_kernel_spmd

F32 = mybir.dt.float32
BF16 = mybir.dt.bfloat16
AF = mybir.ActivationFunctionType
ALU = mybir.AluOpType
AX = mybir.AxisListType

D = 1024
T = 2304
TL = 2048
TC = 256
DEPTH = 4
EPS = 1e-6
TILES = [(0, 512), (512, 512), (1024, 512), (1536, 512), (2048, 256)]
NSLOT = 3
NDS = 24
NDS_SW = 16
NPP = 80

ENGS = ("pe", "act", "dve", "pool", "sp")


class Op:
    __slots__ = ("eng", "fn", "dma", "deps", "marked", "rank", "dsem", "dval", "prev_dval")

    def __init__(self, eng, fn, dma):
        self.eng = eng
        self.fn = fn
        self.dma = dma
        self.deps = set()
        self.marked = False
        self.rank = 0
        self.dsem = -1
        self.dval = 0
        self.prev_dval = 0


class Prog:
    def __init__(self):
        self.ops = {e: [] for e in ENGS}
        self.lastw = {}
        self.readers = {}
        self.ndma = 0
        self.ndma_sw = 0
        self.final = []

    def add(self, eng, fn, reads=(), writes=(), dma=False):
        op = Op(eng, fn, dma)
        deps = set()
        for k in reads:
            w = self.lastw.get(k)
            if w is not None:
                deps.add(w)
        for k in writes:
            w = self.lastw.get(k)
            if w is not None:
                deps.add(w)
            for r in self.readers.get(k, ()):
                deps.add(r)
        for k in reads:
            self.readers.setdefault(k, []).append(op)
        for k in writes:
            self.lastw[k] = op
            self.readers[k] = []
        deps.discard(op)
        for d in deps:
            if d.dma:
                op.deps.add(d)
            elif d.eng == eng and eng == "pe" and not dma:
                continue
            else:
                op.deps.add(d)
                d.marked = True
        if dma:
            if eng == "pool":
                n = self.ndma_sw
                self.ndma_sw += 1
                base, cntp = 0, NDS_SW
            else:
                n = self.ndma
                self.ndma += 1
                base, cntp = NDS_SW, NDS - NDS_SW
            op.dsem = base + n % cntp
            op.dval = 16 * (n // cntp + 1)
            op.prev_dval = 16 * (n // cntp)
        self.ops[eng].append(op)
        return op

    def emit(self, handles, esem, dsems):
        for e in ENGS:
            r = 0
            for op in self.ops[e]:
                if not op.dma and op.marked:
                    r += 1
                    op.rank = r
        for e in ENGS:
            h = handles[e]
            known = {}
            for op in self.ops[e]:
                need = {}
                for d in op.deps:
                    if d.dma:
                        key = ("d", d.dsem)
                        val = d.dval
                    else:
                        key = ("e", d.eng)
                        val = d.rank
                    if need.get(key, 0) < val:
                        need[key] = val
                if op.dma and op.prev_dval > 0:
                    key = ("d", op.dsem)
                    if need.get(key, 0) < op.prev_dval:
                        need[key] = op.prev_dval
                for key, val in need.items():
                    if known.get(key, 0) >= val:
                        continue
                    known[key] = val
                    sem = dsems[key[1]] if key[0] == "d" else esem[key[1]]
                    h.wait_ge(sem, val)
                ins = op.fn(h)
                if op.dma:
                    ins.then_inc(dsems[op.dsem], 16)
                elif op.marked:
                    ins.then_inc(esem[e], 1)
            for (fe, dop) in self.final:
                if fe == e:
                    h.wait_ge(dsems[dop.dsem], dop.dval)


def build(L, dbg=None):
    dbg = dbg or {}
    MIX = dbg.get('mix', 'BACD')
    DOFFN = dbg.get('ffn', True)
    nc = bass.Bass("TRN2", target_bir_lowering=False)
    dt_in = lambda name, shape: nc.dram_tensor(name, shape, F32, kind="ExternalInput").ap()
    hT0 = dt_in("hT0", [D, T])
    cvec = dt_in("cvec", [128, 16])
    w_ada = dt_in("w_ada", [L, D, 6 * D])
    w_in = dt_in("w_in", [L, D, 2048])
    w_out = dt_in("w_out", [L, D, D])
    w_ff1 = dt_in("w_ff1", [L, D, 4 * D])
    w_ff2 = dt_in("w_ff2", [L, 4 * D, D])
    ppd = dt_in("pp", [L, 128, NPP])
    sgnd = dt_in("sgn", [L, 128, 256])
    bstd = dt_in("bst", [L, 128, 256])
    wstd = dt_in("wst", [L, 128, 512])
    wpd = dt_in("wpool", [L, 4, 64, 64])
    cosd = dt_in("cosd", [128, TL])
    sind = dt_in("sind", [128, TL])
    cstd = dt_in("cstd", [128, 1024])
    rced = dt_in("rced", [128, 36])
    outT = nc.dram_tensor("outT", [D, T], F32, kind="ExternalOutput").ap()

    P = Prog()
    SW = 2336
    PHW = 15168

    import contextlib
    with contextlib.ExitStack() as es:
        sb = lambda name, shape, dt: es.enter_context(nc.sbuf_tensor("s_" + name, shape, dt))
        hT = sb("hT", [128, 8, T], F32)
        aT = sb("aT", [128, 8, T], BF16)
        wr = sb("wr", [128, NSLOT, 4096], BF16)
        cosT = sb("cosT", [128, TL], BF16)
        sinT = sb("sinT", [128, TL], BF16)
        cst = sb("cst", [128, 1024], BF16)
        pp = sb("pp", [128, NPP], F32)
        sgn = sb("sgn", [128, 256], F32)
        bst = sb("bst", [128, 2, 128], F32)
        wst = sb("wst", [128, 4, 128], BF16)
        wpb = sb("wpb", [128, 2, 128], BF16)
        cv = sb("cv", [128, 16], F32)
        svec = sb("svec", [128, 8, 2], BF16)
        modT = sb("modT", [128, 2, 48, 2], F32)
        der = sb("der", [128, 2, 2, 6, 8], F32)
        sm = sb("sm", [128, 16], F32)
        rce = sb("rce", [128, 36], F32)
        ssq = sb("ssq", [128, 12], F32)
        epsc = sb("epsc", [128, 4], F32)
        ph = sb("ph", [128, PHW], F32)
        ps = es.enter_context(nc.psum_tensor("ps", [128, 8, 512], F32))
        esem = {e: es.enter_context(nc.semaphore("sem_" + e)) for e in ENGS}
        dsems = [es.enter_context(nc.semaphore("dsem%d" % i)) for i in range(NDS)]

        S0 = ph[:, 0:SW]
        S1 = ph[:, SW:2 * SW]
        R1o = 2 * SW
        qT = ph[:, R1o:R1o + 2304].bitcast(BF16).rearrange("p (c t) -> p c t", c=2)
        kT = ph[:, R1o + 2304:R1o + 3456].bitcast(BF16)
        Vt = ph[:, R1o + 3456:R1o + 4608].bitcast(BF16).rearrange("p (b c) -> p b c", b=18)
        S2 = ph[:, R1o:R1o + SW]
        dT = ph[:, R1o + SW:R1o + SW + SW // 2].bitcast(BF16)
        Yo = R1o + 4608
        Y = ph[:, Yo:Yo + 2304].bitcast(BF16).rearrange("p (c t) -> p c t", c=2)
        TPo = Yo + 2304
        TP = [ph[:, TPo + i * 512:TPo + (i + 1) * 512] for i in range(5)]
        SQo = TPo + 5 * 512
        SQ = [ph[:, SQo + i * 256:SQo + (i + 1) * 256].bitcast(BF16) for i in range(2)]
        PTo = SQo + 2 * 256
        PT = [ph[:, PTo + i * 128:PTo + (i + 1) * 128].bitcast(BF16) for i in range(2)]
        VNo = PTo + 2 * 128
        VN = [ph[:, VNo + i * 128:VNo + (i + 1) * 128].bitcast(BF16) for i in range(2)]
        assert VNo + 256 <= PHW, VNo + 256
        gT = ph[:, 0:4608].bitcast(BF16).rearrange("p (c t) -> p c t", c=4)
        RT = [TP[3], TP[4]]

        ones = cst[:, 0:128]
        bones = cst[:, 128:256]
        Rm = cst[:, 256:384]
        ident = cst[:, 384:512]
        mprev = cst[:, 512:768]
        mnext = cst[:, 768:1024]

        cnt = {"dense": 0, "misc": 0, "tok": 0, "slot": 0}

        def dense_bank():
            b = cnt["dense"] % 4
            cnt["dense"] += 1
            return b

        def misc_bank():
            b = 4 + cnt["misc"] % 2
            cnt["misc"] += 1
            return b

        def tok_bank():
            b = 6 + cnt["tok"] % 2
            cnt["tok"] += 1
            return b

        free_slots = list(range(NSLOT))
        cfg = {"nt": 5}

        def alloc_slot():
            assert free_slots, "weight ring exhausted"
            return free_slots.pop(0)

        def release_slot(sl):
            assert sl not in free_slots
            free_slots.append(sl)

        def pk(b):
            return ("ps", b)

        for k in range(8):
            P.add("sp", lambda h, k=k: h.dma_start(out=hT[:, k, :], in_=hT0[k * 128:(k + 1) * 128, :]),
                  writes=[("hT", k, ti) for ti in range(5)], dma=True)
        P.add("sp", lambda h: h.dma_start(out=cv[:], in_=cvec[:, :]), writes=["cv"], dma=True)
        P.add("sp", lambda h: h.dma_start(out=rce[:], in_=rced[:, :]), writes=["rce"], dma=True)
        P.add("pool", lambda h: h.dma_start(out=cosT[:], in_=cosd[:, :]), writes=["cos"], dma=True)
        P.add("pool", lambda h: h.dma_start(out=sinT[:], in_=sind[:, :]), writes=["sin"], dma=True)
        P.add("pool", lambda h: h.dma_start(out=cst[:], in_=cstd[:, :]), writes=["cst"], dma=True)
        P.add("act", lambda h: h.activation(out=svec[:].rearrange("p k i -> p (k i)"), in_=cv[:], func=AF.Silu),
              reads=["cv"], writes=["svec"])
        P.add("dve", lambda h: h.memset(wpb[:].rearrange("p a b -> p (a b)"), 0.0), writes=["wpb"])
        P.add("dve", lambda h: h.memset(epsc[:, 0:1], float(D * EPS)), writes=["epsc"])
        P.add("dve", lambda h: h.memset(epsc[:, 1:2], float(64 * EPS)), writes=["epsc"])
        P.add("dve", lambda h: h.memset(epsc[:, 2:3], float(256 * EPS)), writes=["epsc"])

        def load_slot(src_ap, s, shape3):
            a, b = shape3
            dst = wr[:, s, :].rearrange("p (a b) -> p a b", a=a)
            return P.add("pool", lambda h: h.dma_start(out=dst, in_=src_ap), writes=[("wr", s)], dma=True)

        def load_params(l):
            P.add("sp", lambda h: h.dma_start(out=pp[:], in_=ppd[l]), writes=["pp"], dma=True)
            P.add("sp", lambda h: h.dma_start(out=sgn[:], in_=sgnd[l]), writes=["sgn"], dma=True)
            P.add("sp", lambda h: h.dma_start(out=bst[:].rearrange("p a b -> p (a b)"), in_=bstd[l]),
                  writes=["bst"], dma=True)
            P.add("pool", lambda h: h.dma_start(out=wst[:].rearrange("p a b -> p (a b)"), in_=wstd[l]),
                  writes=["wst"], dma=True)
            for j in range(2):
                P.add("pool", lambda h, j=j: h.dma_start(out=wpb[0:64, j, 0:64], in_=wpd[l, 2 * j]),
                      writes=["wpb"], dma=True)
                P.add("pool", lambda h, j=j: h.dma_start(out=wpb[64:128, j, 64:128], in_=wpd[l, 2 * j + 1]),
                      writes=["wpb"], dma=True)

        def mod_load(l, cg):
            s = alloc_slot()
            src = w_ada[l].rearrange("(k p) n -> p k n", p=128)[:, :, cg * 512:(cg + 1) * 512]
            load_slot(src, s, (8, 512))
            return s

        def mod_mm(s, cg):
            for cc in range(4):
                col = (cg * 4 + cc) * 2
                for k in range(8):
                    P.add("pe", lambda h, s=s, cc=cc, k=k, col=col: h.matmul(
                        ps[:, 5, col:col + 2], lhsT=wr[:, s, k * 512 + cc * 128:k * 512 + (cc + 1) * 128],
                        rhs=svec[:, k, :], start=(k == 0), stop=(k == 7)),
                        reads=[("wr", s), "svec"], writes=[pk(5)])
            release_slot(s)

        def mod_finish(par):
            P.add("dve", lambda h: h.tensor_tensor(
                out=modT[:, par, :, :], in0=ps[:, 5, 0:96].rearrange("p (c i) -> p c i", i=2),
                in1=pp[:, 0:48].unsqueeze(2).broadcast_to([128, 48, 2]), op=ALU.add),
                reads=["pp"], writes=[pk(5), ("modT", par)])
            for i in range(2):
                for (w, scc, shc, gc, nrm) in ((0, 1, 0, 2, 48), (1, 4, 3, 5, 56)):
                    P.add("dve", lambda h, i=i, w=w, scc=scc, nrm=nrm: h.scalar_tensor_tensor(
                        out=der[:, par, i, 3 * w + 0, :], in0=modT[:, par, scc * 8:scc * 8 + 8, i], scalar=1.0,
                        in1=pp[:, nrm:nrm + 8], op0=ALU.add, op1=ALU.mult),
                        reads=[("modT", par), "pp"], writes=[("der", par, i, w, 0)])
                    P.add("dve", lambda h, i=i, w=w: h.tensor_scalar(
                        out=der[:, par, i, 3 * w + 0, :], in0=der[:, par, i, 3 * w + 0, :], scalar1=float(np.sqrt(D)),
                        scalar2=None, op0=ALU.mult),
                        reads=[("der", par, i, w, 0)], writes=[("der", par, i, w, 0)])
                    P.add("dve", lambda h, i=i, w=w, shc=shc: h.tensor_copy(
                        out=der[:, par, i, 3 * w + 1, :], in_=modT[:, par, shc * 8:shc * 8 + 8, i]),
                        reads=[("modT", par)], writes=[("der", par, i, w, 1)])
                    P.add("dve", lambda h, i=i, w=w, gc=gc: h.tensor_copy(
                        out=der[:, par, i, 3 * w + 2, :], in_=modT[:, par, gc * 8:gc * 8 + 8, i]),
                        reads=[("modT", par)], writes=[("der", par, i, w, 2)])

        def layer_small(l):
            P.add("dve", lambda h: h.tensor_scalar(out=sm[:, 0:2], in0=pp[:, 64:66], scalar1=8.0, scalar2=None,
                                                   op0=ALU.mult), reads=["pp"], writes=["sm01"])
            P.add("act", lambda h: h.activation(out=sm[:, 2:4], in_=pp[:, 66:68], func=AF.Exp),
                  reads=["pp"], writes=["sm23"])
            P.add("act", lambda h: h.activation(out=sm[:, 12:16], in_=pp[:, 76:80], func=AF.Exp),
                  reads=["pp"], writes=["sm12"])
            P.add("dve", lambda h: h.tensor_scalar(out=sgn[:], in0=sgn[:], scalar1=16.0, scalar2=None,
                                                   op0=ALU.mult), reads=["sgn"], writes=["sgn"])

        def tile_i(ti):
            return 1 if ti == 4 else 0

        def norm_phase(par, w, nt=5):
            mbs = {}

            SQN = [SQ[0], SQ[1]] + [ph[:, R1o + i * 256:R1o + (i + 1) * 256].bitcast(BF16) for i in range(4)]

            def sq_one(ti, k):
                t0, tn = TILES[ti]
                if k == 0:
                    mbs[ti] = misc_bank()
                mb = mbs[ti]
                sqi = (ti * 8 + k) % 6
                sq = SQN[sqi]
                if k in (1, 3):
                    P.add("pool", lambda h: h.tensor_tensor(
                        out=sq[:, 0:tn], in0=hT[:, k, t0:t0 + tn], in1=hT[:, k, t0:t0 + tn], op=ALU.mult),
                        reads=[("hT", k, ti)], writes=[("SQ", sqi)])
                elif k == 99:
                    P.add("dve", lambda h: h.tensor_tensor(
                        out=sq[:, 0:tn], in0=hT[:, k, t0:t0 + tn], in1=hT[:, k, t0:t0 + tn], op=ALU.mult),
                        reads=[("hT", k, ti)], writes=[("SQ", sqi)])
                else:
                    P.add("act", lambda h: h.activation(
                        out=sq[:, 0:tn], in_=hT[:, k, t0:t0 + tn], func=AF.Square),
                        reads=[("hT", k, ti)], writes=[("SQ", sqi)])
                P.add("pe", lambda h: h.matmul(
                    ps[:, mb, 0:tn], lhsT=ones, rhs=sq[:, 0:tn], start=(k == 0), stop=(k == 7)),
                    reads=[("SQ", sqi), "cst"], writes=[pk(mb)])

            def rstd_part(ti):
                t0, tn = TILES[ti]
                mb = mbs[ti]
                rs = ti % 2
                P.add("act", lambda h: h.activation(
                    out=TP[rs][:, 0:tn], in_=ps[:, mb, 0:tn], func=AF.Ln, bias=epsc[:, 0:1], scale=1.0),
                    reads=["epsc"], writes=[pk(mb), ("TP", rs)])
                P.add("act", lambda h: h.activation(
                    out=TP[rs][:, 0:tn], in_=TP[rs][:, 0:tn], func=AF.Exp, scale=-0.5),
                    reads=[("TP", rs)], writes=[("TP", rs)])

            def apply_one(ti, k):
                t0, tn = TILES[ti]
                i = tile_i(ti)
                rs = ti % 2
                tp = 2 + k % 3
                eng = "pool" if k in (1, 3) else "dve"
                P.add(eng, lambda h: h.tensor_tensor(
                    out=TP[tp][:, 0:tn], in0=hT[:, k, t0:t0 + tn], in1=TP[rs][:, 0:tn], op=ALU.mult),
                    reads=[("hT", k, ti), ("TP", rs)], writes=[("TP", tp)])
                if k % 2 == 0:
                    P.add("act", lambda h: h.activation(
                        out=aT[:, k, t0:t0 + tn], in_=TP[tp][:, 0:tn], func=AF.Identity,
                        bias=der[:, par, i, 3 * w + 1, k:k + 1], scale=der[:, par, i, 3 * w + 0, k:k + 1]),
                        reads=[("TP", tp), ("der", par, i, w, 0), ("der", par, i, w, 1)], writes=[("aT", k, ti)])
                else:
                    P.add("dve", lambda h: h.tensor_scalar(
                        out=aT[:, k, t0:t0 + tn], in0=TP[tp][:, 0:tn],
                        scalar1=der[:, par, i, 3 * w + 0, k:k + 1], scalar2=der[:, par, i, 3 * w + 1, k:k + 1],
                        op0=ALU.mult, op1=ALU.add),
                        reads=[("TP", tp), ("der", par, i, w, 0), ("der", par, i, w, 1)], writes=[("aT", k, ti)])

            for k in range(8):
                sq_one(0, k)
            for ti in range(nt):
                rstd_part(ti)
                for k in range(8):
                    if ti + 1 < nt and k < 4:
                        sq_one(ti + 1, 2 * k)
                        sq_one(ti + 1, 2 * k + 1)
                    apply_one(ti, k)

        def mm8(b, s, c, ti):
            t0, tn = TILES[ti]
            for k in range(8):
                P.add("pe", lambda h, k=k: h.matmul(
                    ps[:, b, 0:tn], lhsT=wr[:, s, k * 512 + c * 128:k * 512 + (c + 1) * 128],
                    rhs=aT[:, k, t0:t0 + tn], start=(k == 0), stop=(k == 7)),
                    reads=[("wr", s), ("aT", k, ti)], writes=[pk(b)])

        def wout_partial(par, s, half, tile_outer=False):
            order = [(m, ti) for m in range(8) for ti in range(cfg["nt"])]
            if tile_outer:
                order = [(m, ti) for ti in range(cfg["nt"]) for m in range(8)]
            for (m, ti) in order:
                for (t0, tn) in (TILES[ti],):
                    i = tile_i(ti)
                    b = dense_bank()
                    for kk in range(2):
                        off = (half * 2 + kk) * 1024 + m * 128
                        P.add("pe", lambda h, kk=kk, off=off, b=b, t0=t0, tn=tn: h.matmul(
                            ps[:, b, 0:tn], lhsT=wr[:, s, off:off + 128], rhs=Y[:, kk, t0:t0 + tn],
                            start=(kk == 0), stop=(kk == 1)),
                            reads=[("wr", s), ("Y", kk, ti)], writes=[pk(b)])
                    P.add("dve", lambda h, m=m, b=b, t0=t0, tn=tn, i=i: h.scalar_tensor_tensor(
                        out=hT[:, m, t0:t0 + tn], in0=ps[:, b, 0:tn], scalar=der[:, par, i, 2, m:m + 1],
                        in1=hT[:, m, t0:t0 + tn], op0=ALU.mult, op1=ALU.add),
                        reads=[("der", par, i, 0, 2)], writes=[pk(b), ("hT", m, ti)])

        def acol(t):
            return 1 + t if t < TL else 2051 + (t - TL)

        def pcol(t):
            return 8 + t if t < TL else 2072 + (t - TL)

        def mixer_phase(l, par):
            wl = w_in[l].rearrange("(k p) n -> p k n", p=128)
            s0 = alloc_slot()
            load_slot(wl[:, :, 0:512], s0, (8, 512))
            def qk_iter(it, c, ti):
                gcol = 0 if c < 2 else 1
                t0, tn = TILES[ti]
                u = it % 2
                base = (S0, S1)[u]
                A0 = base[:, 0:512]
                A1 = base[:, 512:1024]
                A2 = base[:, 1024:1536]
                A3 = base[:, 1536:2048]
                QB = base[:, 2048:2304].bitcast(BF16)
                qk = lambda i: ("QK", u, i)
                st = {}

                def part0():
                    st["b"] = dense_bank()
                    mm8(st["b"], s0, c, ti)

                def part1():
                    b = st["b"]
                    P.add("act", lambda h: h.activation(
                        out=A0[:, 0:tn], in_=ps[:, b, 0:tn], func=AF.Identity, scale=sm[:, gcol:gcol + 1]),
                        reads=["sm01"], writes=[pk(b), qk(0)])
                    if ti < 4:
                        P.add("dve", lambda h: h.tensor_scalar(
                            out=QB[:, 0:tn], in0=ps[:, b, 0:tn], scalar1=sm[:, gcol:gcol + 1], scalar2=None, op0=ALU.mult),
                            reads=["sm01"], writes=[pk(b), qk(4)])
                    P.add("act", lambda h: h.activation(out=SQ[u][:, 0:tn], in_=ps[:, b, 0:tn], func=AF.Square),
                          reads=[], writes=[pk(b), ("SQ", u)])
                    mb = misc_bank()
                    st["mb"] = mb
                    P.add("pe", lambda h: h.matmul(ps[:, mb, 0:tn], lhsT=bones, rhs=SQ[u][:, 0:tn], start=True, stop=True),
                          reads=[("SQ", u), "cst"], writes=[pk(mb)])
                    if ti < 4:
                        mb2 = tok_bank()
                        st["mb2"] = mb2
                        P.add("pe", lambda h: h.matmul(ps[:, mb2, 0:tn], lhsT=Rm, rhs=QB[:, 0:tn], start=True, stop=True),
                              reads=[qk(4), "cst"], writes=[pk(mb2)])

                def part2():
                    mb = st["mb"]
                    P.add("act", lambda h: h.activation(
                        out=A1[:, 0:tn], in_=ps[:, mb, 0:tn], func=AF.Ln, bias=epsc[:, 1:2], scale=1.0),
                        reads=["epsc"], writes=[pk(mb), qk(1)])
                    P.add("act", lambda h: h.activation(
                        out=A1[:, 0:tn], in_=A1[:, 0:tn], func=AF.Exp, scale=-0.5),
                        reads=[qk(1)], writes=[qk(1)])
                    if c < 2:
                        dst = qT[:, c, t0:t0 + tn]
                        dkey = ("qT", c, ti)
                    else:
                        dst = kT[:, t0:t0 + tn]
                        dkey = ("kT", ti)
                    if ti < 4:
                        mb2 = st["mb2"]
                        P.add("pool", lambda h: h.tensor_tensor(
                            out=A2[:, 0:tn], in0=A0[:, 0:tn], in1=cosT[:, t0:t0 + tn], op=ALU.mult),
                            reads=[qk(0), "cos"], writes=[qk(2)])
                        P.add("dve", lambda h: h.tensor_tensor(
                            out=A3[:, 0:tn], in0=ps[:, mb2, 0:tn], in1=sinT[:, t0:t0 + tn], op=ALU.mult),
                            reads=["sin"], writes=[pk(mb2), qk(3)])
                        P.add("pool", lambda h: h.tensor_tensor(
                            out=A2[:, 0:tn], in0=A2[:, 0:tn], in1=A3[:, 0:tn], op=ALU.add),
                            reads=[qk(2), qk(3)], writes=[qk(2)])
                        P.add("dve", lambda h: h.tensor_tensor(
                            out=dst, in0=A2[:, 0:tn], in1=A1[:, 0:tn], op=ALU.mult),
                            reads=[qk(2), qk(1)], writes=[dkey])
                    else:
                        P.add("dve", lambda h: h.tensor_tensor(
                            out=dst, in0=A0[:, 0:tn], in1=A1[:, 0:tn], op=ALU.mult),
                            reads=[qk(0), qk(1)], writes=[dkey])
                return part0, part1, part2

            iters = []
            for c in range(3):
                for ti in range(5):
                    if cfg["nt"] == 4 and c < 2 and ti == 4:
                        continue
                    iters.append(qk_iter(len(iters), c, ti))
            NI = len(iters)
            iters[0][0]()
            iters[1][0]()
            iters[0][1]()
            for ii in range(NI):
                if ii + 2 < NI:
                    iters[ii + 2][0]()
                if ii + 1 < NI:
                    iters[ii + 1][1]()
                iters[ii][2]()
            for blk in range(18):
                tb = tok_bank()
                ti = min(blk // 4, 4)
                for k in range(8):
                    P.add("pe", lambda h, k=k, blk=blk, tb=tb: h.matmul(
                        ps[:, tb, 0:128], lhsT=aT[:, k, blk * 128:(blk + 1) * 128],
                        rhs=wr[:, s0, k * 512 + 384:k * 512 + 512], start=(k == 0), stop=(k == 7)),
                        reads=[("wr", s0), ("aT", k, ti)], writes=[pk(tb)])
                P.add("act", lambda h, blk=blk, tb=tb: h.activation(out=Vt[:, blk, :], in_=ps[:, tb, 0:128], func=AF.Copy),
                      reads=[], writes=[pk(tb), ("V", blk)])

            release_slot(s0)
            so0 = alloc_slot()
            load_slot(w_out[l, 0:512, :].rearrange("(a p) n -> p a n", p=128), so0, (4, 1024))

            PT2 = [ph[:, PTo + i * 256:PTo + (i + 1) * 256].bitcast(BF16) for i in range(2)]
            PT2 += [TP[3][:, i * 256:(i + 1) * 256].bitcast(BF16) for i in range(2)]
            QX = [TP[2][:, i * 256:(i + 1) * 256].bitcast(BF16) for i in range(2)]
            for i in range(2):
                P.add("pool", lambda h, i=i: h.memset(QX[i][:, 0:512], 0.0),
                      writes=[("QX", i)] + [("TP", x) for x in range(5)])
            steps = []
            NQB = 16 if cfg["nt"] == 4 else 18
            for n in range(NQB):
                if n < 16:
                    kcs = []
                    if n > 0:
                        kcs.append((n - 1, mprev))
                    kcs.append((n, None))
                    if n < 15:
                        kcs.append((n + 1, mnext))
                    kcs += [(16, None), (17, None)]
                else:
                    kcs = [(16, None), (17, None)]
                for ci, (kc, mask) in enumerate(kcs):
                    steps.append(dict(n=n, ci=ci, kc=kc, mask=mask, last=(ci == len(kcs) - 1)))

            def emit_qx(n):
                nti = min(n // 4, 4)
                qx = QX[n % 2]
                P.add("pool", lambda h: h.tensor_copy(
                    out=qx[0:64, 0:256].rearrange("p (c t) -> p c t", c=2), in_=qT[0:64, :, n * 128:(n + 1) * 128]),
                    reads=[("qT", 0, nti), ("qT", 1, nti)], writes=[("QX", n % 2)])
                P.add("pool", lambda h: h.tensor_copy(
                    out=qx[64:128, 256:512].rearrange("p (c t) -> p c t", c=2), in_=qT[64:128, :, n * 128:(n + 1) * 128]),
                    reads=[("qT", 0, nti), ("qT", 1, nti)], writes=[("QX", n % 2)])

            def emit_S(st):
                n, kc, mask = st["n"], st["kc"], st["mask"]
                kti = min(kc // 4, 4)
                sbk = (4, 5, 2, 3)[st["idx"] % 4]
                st["sbk"] = sbk
                st["pti"] = st["idx"] % 4
                qx = QX[n % 2]
                P.add("pe", lambda h: h.matmul(
                    ps[:, sbk, 0:512], lhsT=kT[:, kc * 128:(kc + 1) * 128], rhs=qx[:, 0:512],
                    start=True, stop=(mask is None)),
                    reads=[("kT", kti), ("QX", n % 2)], writes=[pk(sbk)])
                if mask is not None:
                    P.add("pe", lambda h: h.matmul(
                        ps[:, sbk, 0:512].rearrange("p (a t) -> p a t", a=2), lhsT=ident,
                        rhs=mask.unsqueeze(1).broadcast_to([128, 2, 256]), start=False, stop=True),
                        reads=["cst"], writes=[pk(sbk)])

            def emit_rest(st):
                n, kc, ci, last = st["n"], st["kc"], st["ci"], st["last"]
                sbk = st["sbk"]
                pt = PT2[st["pti"]]
                ptk = ("PT", st["pti"])
                ob, db = (6, 7) if n % 2 == 0 else (0, 1)
                P.add("pe", lambda h: h.matmul(
                    ps[:, ob, 0:512], lhsT=Vt[:, kc, :], rhs=pt[:, 0:512], start=(ci == 0), stop=last),
                    reads=[("V", kc), ptk], writes=[pk(ob)])
                P.add("pe", lambda h: h.matmul(
                    ps[:, db, 0:512], lhsT=ones, rhs=pt[:, 0:512], start=(ci == 0), stop=last),
                    reads=["cst", ptk], writes=[pk(db)])
                if last:
                    fin_q.append((st, 0))
                    if dbg.get("nodefer"):
                        while fin_q:
                            pump_fin()

            fin_q = []

            def emit_fin(st, stage):
                n = st["n"]
                ob, db = (6, 7) if n % 2 == 0 else (0, 1)
                nti = min(n // 4, 4)
                tf = TP[n % 2]
                tfk = ("TP", n % 2)
                if stage == 0:
                    P.add("dve", lambda h: h.tensor_tensor(
                        out=tf[:, 0:512].rearrange("p (c t) -> p c t", c=4),
                        in0=ps[:, db, 0:512].rearrange("p (c t) -> p c t", c=4),
                        in1=sm[:, 12:16].unsqueeze(2).broadcast_to([128, 4, 128]), op=ALU.add),
                        reads=["sm12"], writes=[pk(db), tfk])
                elif stage == 1:
                    P.add("act", lambda h: h.activation(out=tf[:, 0:512], in_=tf[:, 0:512], func=AF.Ln),
                          reads=[tfk], writes=[tfk])
                    P.add("act", lambda h: h.activation(out=tf[:, 0:512], in_=tf[:, 0:512], func=AF.Exp, scale=-1.0),
                          reads=[tfk], writes=[tfk])
                else:
                    P.add("dve", lambda h: h.tensor_tensor(
                        out=Y[0:64, :, n * 128:(n + 1) * 128],
                        in0=ps[0:64, ob, 0:256].rearrange("p (c t) -> p c t", c=2),
                        in1=tf[0:64, 0:256].rearrange("p (c t) -> p c t", c=2), op=ALU.mult),
                        reads=[tfk], writes=[pk(ob), ("Y", 0, nti), ("Y", 1, nti)])
                    P.add("dve", lambda h: h.tensor_tensor(
                        out=Y[64:128, :, n * 128:(n + 1) * 128],
                        in0=ps[64:128, ob, 256:512].rearrange("p (c t) -> p c t", c=2),
                        in1=tf[64:128, 256:512].rearrange("p (c t) -> p c t", c=2), op=ALU.mult),
                        reads=[tfk], writes=[pk(ob), ("Y", 0, nti), ("Y", 1, nti)])

            def emit_exp(st):
                sbk = st["sbk"]
                pt = PT2[st["pti"]]
                P.add("act", lambda h: h.activation(
                    out=pt[:, 0:512], in_=ps[:, sbk, 0:512], func=AF.Exp, scale=0.125),
                    reads=[], writes=[pk(sbk), ("PT", st["pti"])])

            for si, st in enumerate(steps):
                st["idx"] = si
            emit_qx(0)
            emit_qx(1)
            emit_S(steps[0])
            emit_S(steps[1])
            def pump_fin():
                nq = []
                for (fst, stage) in fin_q:
                    emit_fin(fst, stage)
                    if stage < 2:
                        nq.append((fst, stage + 1))
                fin_q[:] = nq

            for si, st in enumerate(steps):
                emit_exp(st)
                pump_fin()
                if si + 2 < len(steps):
                    emit_S(steps[si + 2])
                emit_rest(st)
                if st["last"] and st["n"] + 2 < NQB:
                    emit_qx(st["n"] + 2)
            while fin_q:
                pump_fin()
            if 'B' in MIX:
                wout_partial(par, so0, 0)

            s1 = alloc_slot()
            load_slot(wl[:, :, 512:1024], s1, (8, 512))
            s2 = alloc_slot()
            load_slot(wl[:, :, 1024:1536], s2, (8, 512))
            amap = {0: ((s1, 0), (s1, 1), (s1, 2)), 1: ((s1, 3), (s2, 0), (s2, 1))}
            for j in range(2):
                (sh, ch), (sc, cc), (sg, cg) = amap[j]
                for ti, (t0, tn) in enumerate(TILES[:cfg["nt"]]):
                    b = dense_bank()
                    mm8(b, sh, ch, ti)
                    a0 = acol(t0)
                    P.add("act", lambda h, b=b, a0=a0, tn=tn: h.activation(out=S0[:, a0:a0 + tn], in_=ps[:, b, 0:tn], func=AF.Copy),
                          reads=[], writes=[pk(b), ("S0", ti)])
                for (c0, c1) in ((0, 1), (2049, 2051), (2307, 2308)):
                    P.add("pool", lambda h, c0=c0, c1=c1: h.memset(S1[:, c0:c1], 0.0), writes=[("S1", "pad")])
                for ti, (t0, tn) in enumerate(TILES[:cfg["nt"]]):
                    b = dense_bank()
                    mm8(b, sc, cc, ti)
                    a0 = acol(t0)
                    P.add("dve", lambda h, b=b, a0=a0, tn=tn: h.tensor_tensor(
                        out=S1[:, a0:a0 + tn], in0=ps[:, b, 0:tn], in1=S0[:, a0:a0 + tn], op=ALU.mult),
                        reads=[("S0", ti)], writes=[pk(b), ("S1", ti)])
                allS0 = [("S0", ti) for ti in range(5)]
                allS1 = [("S1", ti) for ti in range(5)] + [("S1", "pad")]
                cw = 6 + j * 3
                P.add("act", lambda h, cw=cw: h.activation(
                    out=S0[:, 1:2307], in_=S1[:, 1:2307], func=AF.Identity, scale=sm[:, cw + 1:cw + 2]),
                    reads=allS1 + ["smconv"], writes=allS0)
                P.add("dve", lambda h, cw=cw: h.scalar_tensor_tensor(
                    out=S0[:, 1:2307], in0=S1[:, 0:2306], scalar=sm[:, cw:cw + 1], in1=S0[:, 1:2307],
                    op0=ALU.mult, op1=ALU.add), reads=allS1 + ["smconv"], writes=allS0)
                P.add("dve", lambda h, cw=cw: h.scalar_tensor_tensor(
                    out=S0[:, 1:2307], in0=S1[:, 2:2308], scalar=sm[:, cw + 2:cw + 3], in1=S0[:, 1:2307],
                    op0=ALU.mult, op1=ALU.add), reads=allS1 + ["smconv"], writes=allS0)
                for ti, (t0, tn) in enumerate(TILES[:cfg["nt"]]):
                    b = dense_bank()
                    mm8(b, sg, cg, ti)
                    a0 = acol(t0)
                    P.add("dve", lambda h, b=b, a0=a0, t0=t0, tn=tn, j=j: h.tensor_tensor(
                        out=Y[:, j, t0:t0 + tn], in0=ps[:, b, 0:tn], in1=S0[:, a0:a0 + tn], op=ALU.mult),
                        reads=[("S0", ti)], writes=[pk(b), ("Y", j, ti)])
            if 'A' in MIX:
                wout_partial(par, so0, 1)

            release_slot(so0)
            release_slot(s1)
            so1 = alloc_slot()
            load_slot(w_out[l, 512:1024, :].rearrange("(a p) n -> p a n", p=128), so1, (4, 1024))
            SU = (S0, S1)
            for j in range(2):
                for ti, (t0, tn) in enumerate(TILES[:cfg["nt"]]):
                    b = dense_bank()
                    mm8(b, s2, 2 + j, ti)
                    P.add("act", lambda h, b=b, t0=t0, tn=tn, j=j: h.activation(
                        out=SU[j][:, t0:t0 + tn], in_=ps[:, b, 0:tn], func=AF.Copy),
                        reads=[], writes=[pk(b), ("S%d" % j, ti)])
            release_slot(s2)
            s3 = alloc_slot()
            load_slot(wl[:, :, 1536:2048], s3, (8, 512))
            def c_bufs(blk):
                o = R1o + (blk % 3) * 768
                return (ph[:, o:o + 256], ph[:, o + 256:o + 512], ph[:, o + 512:o + 640].bitcast(BF16), (blk % 3))

            c_tb = {}

            def c_stage0(blk):
                tb = tok_bank()
                c_tb[blk] = tb
                ti = min(blk // 4, 4)
                for k in range(8):
                    P.add("pe", lambda h, k=k: h.matmul(
                        ps[:, tb, 0:256], lhsT=aT[:, k, blk * 128:(blk + 1) * 128],
                        rhs=wr[:, s3, k * 512:k * 512 + 256], start=(k == 0), stop=(k == 7)),
                        reads=[("wr", s3), ("aT", k, ti)], writes=[pk(tb)])

            def c_stage1(blk):
                tb = c_tb[blk]
                ti = min(blk // 4, 4)
                raw, sqb, vn, si = c_bufs(blk)
                so = 4 * si
                P.add("act", lambda h: h.activation(out=raw[:, 0:256], in_=ps[:, tb, 0:256], func=AF.Copy),
                      reads=[], writes=[pk(tb), ("Craw", si)])
                P.add("pool", lambda h: h.tensor_tensor(out=sqb[:, 0:256], in0=raw[:, 0:256], in1=raw[:, 0:256], op=ALU.mult),
                      reads=[("Craw", si)], writes=[("Csq", si)])
                P.add("dve", lambda h: h.reduce_sum(out=ssq[:, so:so + 1], in_=sqb[:, 0:256], axis=AX.X),
                      reads=[("Csq", si)], writes=[("ssq", so)])
                P.add("act", lambda h: h.activation(out=ssq[:, so + 2:so + 3], in_=ssq[:, so:so + 1], func=AF.Sqrt,
                                                    bias=epsc[:, 2:3], scale=1.0),
                      reads=[("ssq", so), "epsc"], writes=[("ssq", so + 2)])
                P.add("dve", lambda h: h.reciprocal(out=ssq[:, so + 1:so + 2], in_=ssq[:, so + 2:so + 3]),
                      reads=[("ssq", so + 2)], writes=[("ssq", so + 1)])
                P.add("dve", lambda h: h.scalar_tensor_tensor(
                    out=vn[:, 0:256], in0=raw[:, 0:256], scalar=ssq[:, so + 1:so + 2], in1=sgn[:], op0=ALU.mult, op1=ALU.mult),
                    reads=[("Craw", si), ("ssq", so + 1), "sgn"], writes=[("Cvn", si)])
            def c_stage2(blk):
                ti = min(blk // 4, 4)
                raw, sqb, vn, si = c_bufs(blk)
                for j in range(2):
                    zb = misc_bank()
                    for gg in range(2):
                        g = 2 * j + gg
                        P.add("pe", lambda h, zb=zb, gg=gg, g=g, vn=vn: h.matmul(
                            ps[gg * 64:(gg + 1) * 64, zb, 0:128], lhsT=vn[:, g * 64:(g + 1) * 64], rhs=wst[:, g, :],
                            start=True, stop=True), reads=[("Cvn", si), "wst"], writes=[pk(zb)])
                    P.add("dve", lambda h, zb=zb, j=j: h.tensor_tensor(
                        out=TP[4][:, j * 128:(j + 1) * 128], in0=ps[:, zb, 0:128], in1=bst[:, j, :], op=ALU.add),
                        reads=["bst"], writes=[pk(zb), ("TP4", j)])
                    P.add("pool", lambda h, j=j, blk=blk: h.tensor_tensor(
                        out=Y[:, j, blk * 128:(blk + 1) * 128], in0=TP[4][:, j * 128:(j + 1) * 128],
                        in1=SU[j][:, blk * 128:(blk + 1) * 128], op=ALU.mult),
                        reads=[("TP4", j), ("S%d" % j, ti)], writes=[("Y", j, ti)])
            NBK = 16 if cfg["nt"] == 4 else 18
            c_stage0(0)
            c_stage1(0)
            c_stage0(1)
            c_stage1(1)
            for blk in range(NBK):
                if blk + 2 < NBK:
                    c_stage0(blk + 2)
                c_stage2(blk)
                if blk + 2 < NBK:
                    c_stage1(blk + 2)
            if 'C' in MIX:
                wout_partial(par, so1, 0)

            allS = lambda nm: [(nm, ti) for ti in range(5)] + [(nm, "pad")]
            for j in range(2):
                for ti, (t0, tn) in enumerate(TILES[:cfg["nt"]]):
                    b = dense_bank()
                    mm8(b, s3, 2 + j, ti)
                    p0 = pcol(t0)
                    P.add("act", lambda h, b=b, p0=p0, tn=tn: h.activation(out=S0[:, p0:p0 + tn], in_=ps[:, b, 0:tn], func=AF.Copy),
                          reads=[], writes=[pk(b), ("S0", ti)])
                for (c0, c1) in ((0, 8), (2056, 2072), (2328, 2336)):
                    P.add("pool", lambda h, c0=c0, c1=c1: h.memset(S0[:, c0:c1], 0.0), writes=[("S0", "pad")])
                P.add("dve", lambda h: h.tensor_tensor(out=S1[:, 1:2335], in0=S0[:, 0:2334], in1=S0[:, 1:2335], op=ALU.add),
                      reads=allS("S0"), writes=allS("S1"))
                if j == 0:
                    P.add("dve", lambda h: h.tensor_tensor(out=S2[64:128, 2:2334], in0=S1[64:128, 1:2333],
                                                            in1=S1[64:128, 3:2335], op=ALU.add),
                          reads=allS("S1"), writes=["S2hi"])
                    P.add("dve", lambda h: h.tensor_copy(out=S2[0:64, 8:2328], in_=S1[0:64, 8:2328]),
                          reads=allS("S1"), writes=["S2lo"])
                else:
                    P.add("dve", lambda h: h.tensor_tensor(out=S2[:, 2:2334], in0=S1[:, 1:2333], in1=S1[:, 3:2335],
                                                            op=ALU.add), reads=allS("S1"), writes=["S2hi", "S2lo"])
                    P.add("dve", lambda h: h.tensor_tensor(out=S1[:, 4:2332], in0=S2[:, 2:2330], in1=S2[:, 6:2334],
                                                           op=ALU.add), reads=["S2hi", "S2lo"], writes=allS("S1"))
                    P.add("dve", lambda h: h.tensor_tensor(out=S2[64:128, 8:2328], in0=S1[64:128, 4:2324],
                                                            in1=S1[64:128, 12:2332], op=ALU.add),
                          reads=allS("S1"), writes=["S2hi"])
                    P.add("dve", lambda h: h.tensor_copy(out=S2[0:64, 8:2328], in_=S1[0:64, 8:2328]),
                          reads=allS("S1"), writes=["S2lo"])
                P.add("dve", lambda h, j=j: h.scalar_tensor_tensor(
                    out=dT[:, 8:2328], in0=S2[:, 8:2328], scalar=rce[:, 32 + j:33 + j], in1=S0[:, 8:2328],
                    op0=ALU.mult, op1=ALU.subtract), reads=["S2hi", "S2lo", "rce"] + allS("S0"), writes=["dT"])
                for (base, tlen) in ((8, TL), (2072, TC)):
                    for side in range(2):
                        c0 = base if side == 0 else base + tlen - 8
                        ro = j * 16 + side * 8
                        P.add("pool", lambda h, c0=c0, ro=ro: h.tensor_tensor(
                            out=TP[0][:, 0:8], in0=S2[:, c0:c0 + 8], in1=rce[:, ro:ro + 8], op=ALU.mult),
                            reads=["S2hi", "S2lo", "rce"], writes=[("TP", 0)])
                        P.add("pool", lambda h, c0=c0: h.tensor_tensor(
                            out=dT[:, c0:c0 + 8], in0=TP[0][:, 0:8], in1=S0[:, c0:c0 + 8], op=ALU.subtract),
                            reads=[("TP", 0)] + allS("S0"), writes=["dT"])
                for ti, (t0, tn) in enumerate(TILES[:cfg["nt"]]):
                    mb = misc_bank()
                    p0 = pcol(t0)
                    P.add("pe", lambda h, mb=mb, p0=p0, tn=tn, j=j: h.matmul(
                        ps[:, mb, 0:tn], lhsT=wpb[:, j, :], rhs=dT[:, p0:p0 + tn], start=True, stop=True),
                        reads=["dT", "wpb"], writes=[pk(mb)])
                    P.add("act", lambda h, mb=mb, t0=t0, tn=tn, j=j: h.activation(
                        out=Y[:, j, t0:t0 + tn], in_=ps[:, mb, 0:tn], func=AF.Identity, scale=pp[:, 68 + j:69 + j]),
                        reads=["pp"], writes=[pk(mb), ("Y", j, ti)])
            if 'D' in MIX:
                wout_partial(par, so1, 1, tile_outer=True)
            release_slot(s3)
            release_slot(so1)

        def ffn_phase(l, par, nxt):
            if nxt is not None:
                load_params(nxt)
            w1v = w_ff1[l].rearrange("(k p) n -> p k n", p=128)

            def ld1(g):
                sl = alloc_slot()
                load_slot(w1v[:, :, g * 512:(g + 1) * 512], sl, (8, 512))
                return sl

            def ld2(g):
                sl = alloc_slot()
                load_slot(w_ff2[l, g * 512:(g + 1) * 512, :].rearrange("(a p) n -> p a n", p=128), sl, (4, 1024))
                return sl

            mods = list(range(12)) if nxt is not None else []

            def next_mod():
                if not mods:
                    return None
                cg = mods.pop(0)
                return (mod_load(nxt, cg), cg)

            sa = ld1(0)
            sb_ = ld2(0)
            ma = next_mod()
            mod_done = [nxt is None]
            norm_phase(par, 1, cfg["nt"])
            for g in range(8):
                for jj in range(4):
                    for ti, (t0, tn) in enumerate(TILES[:cfg["nt"]]):
                        b = dense_bank()
                        mm8(b, sa, jj, ti)
                        rt = cnt["dense"] % 2
                        P.add("act", lambda h, b=b, tn=tn, rt=rt: h.activation(out=RT[rt][:, 0:tn], in_=ps[:, b, 0:tn], func=AF.Relu),
                              reads=[], writes=[pk(b), ("TP", 3 + rt)])
                        P.add("pool", lambda h, rt=rt, jj=jj, t0=t0, tn=tn: h.tensor_tensor(
                            out=gT[:, jj, t0:t0 + tn], in0=RT[rt][:, 0:tn], in1=RT[rt][:, 0:tn], op=ALU.mult),
                            reads=[("TP", 3 + rt)], writes=[("gT", jj, ti)])
                release_slot(sa)
                mb_ = next_mod()
                if ma is not None:
                    mod_mm(*ma)
                sa_n = ld1(g + 1) if g + 1 < 8 else None
                order = [(m, ti) for m in range(8) for ti in range(cfg["nt"])]
                if g == 7:
                    order = [(m, ti) for ti in range(cfg["nt"]) for m in range(8)]
                for (m, ti) in order:
                    for (t0, tn) in (TILES[ti],):
                        i = tile_i(ti)
                        b = dense_bank()
                        for jj in range(4):
                            off = jj * 1024 + m * 128
                            P.add("pe", lambda h, jj=jj, off=off, b=b, t0=t0, tn=tn, sb_=sb_: h.matmul(
                                ps[:, b, 0:tn], lhsT=wr[:, sb_, off:off + 128], rhs=gT[:, jj, t0:t0 + tn],
                                start=(jj == 0), stop=(jj == 3)),
                                reads=[("wr", sb_), ("gT", jj, ti)], writes=[pk(b)])
                        P.add("dve", lambda h, m=m, b=b, t0=t0, tn=tn, i=i: h.scalar_tensor_tensor(
                            out=hT[:, m, t0:t0 + tn], in0=ps[:, b, 0:tn], scalar=der[:, par, i, 5, m:m + 1],
                            in1=hT[:, m, t0:t0 + tn], op0=ALU.mult, op1=ALU.add),
                            reads=[("der", par, i, 1, 2)], writes=[pk(b), ("hT", m, ti)])
                release_slot(sb_)
                sb_n = ld2(g + 1) if g + 1 < 8 else None
                if mb_ is not None:
                    mod_mm(*mb_)
                ma = next_mod()
                if not mod_done[0] and not mods and ma is None:
                    mod_finish(1 - par)
                    mod_done[0] = True
                sa, sb_ = sa_n, sb_n

        load_params(0)
        pend = [(mod_load(0, 0), 0), (mod_load(0, 1), 1)]
        for cg in range(12):
            mod_mm(*pend.pop(0))
            if cg + 2 < 12:
                pend.append((mod_load(0, cg + 2), cg + 2))
        mod_finish(0)
        for l in range(L):
            par = l % 2
            if l > 0 and not DOFFN:
                load_params(l)
                mod_finish(par)
            layer_small(l)
            P.add("dve", lambda h: h.tensor_copy(out=sm[:, 6:12], in_=pp[:, 70:76]), reads=["pp"], writes=["smconv"])
            norm_phase(par, 0)
            cfg["nt"] = 4 if (l == L - 1 and L == DEPTH) else 5
            mixer_phase(l, par)
            if DOFFN:
                ffn_phase(l, par, l + 1 if l + 1 < L else None)
        outs = []
        for k in range(8):
            outs.append(P.add("sp", lambda h, k=k: h.dma_start(out=outT[k * 128:(k + 1) * 128, :], in_=hT[:, k, :]),
                              reads=[("hT", k, ti) for ti in range(5)], dma=True))
        for o in outs:
            P.final.append(("sp", o))

        with nc.Block() as block:
            handles = {}

            @block.tensor
            def _(h):
                P_emit_one(P, "pe", h, esem, dsems)

            @block.scalar
            def _(h):
                P_emit_one(P, "act", h, esem, dsems)

            @block.vector
            def _(h):
                P_emit_one(P, "dve", h, esem, dsems)

            @block.gpsimd
            def _(h):
                P_emit_one(P, "pool", h, esem, dsems)

            @block.sync
            def _(h):
                P_emit_one(P, "sp", h, esem, dsems)
    return nc


def P_emit_one(P, e, h, esem, dsems):
    if not getattr(P, "_ranked", False):
        for ee in ENGS:
            r = 0
            for op in P.ops[ee]:
                if not op.dma and op.marked:
                    r += 1
                    op.rank = r
        P._ranked = True
    known = {}
    for op in P.ops[e]:
        need = {}
        for d in op.deps:
            if d.dma:
                key = ("d", d.dsem)
                val = d.dval
            else:
                key = ("e", d.eng)
                val = d.rank
            if need.get(key, 0) < val:
                need[key] = val
        if op.dma and op.prev_dval > 0:
            key = ("d", op.dsem)
            if need.get(key, 0) < op.prev_dval:
                need[key] = op.prev_dval
        for key, val in need.items():
            if known.get(key, 0) >= val:
                continue
            known[key] = val
            sem = dsems[key[1]] if key[0] == "d" else esem[key[1]]
            h.wait_ge(sem, val)
        ins = op.fn(h)
        if op.dma:
            ins.then_inc(dsems[op.dsem], 16)
        elif op.marked:
            ins.then_inc(esem[e], 1)
    for (fe, dop) in P.final:
        if fe == e:
            h.wait_ge(dsems[dop.dsem], dop.dval)


def _consts():
    rows = np.repeat(np.arange(TL // 64), 64).astype(np.float32)
    cols = np.tile(np.arange(64), TL // 64).astype(np.float32)
    inv = (10000.0 ** (-np.arange(16, dtype=np.float32) / 16)).astype(np.float32)
    cosd = np.zeros((128, TL), np.float32)
    sind = np.zeros((128, TL), np.float32)
    for p in range(128):
        d = p % 64
        pos = rows if d < 32 else cols
        ang = pos * inv[d % 16]
        cosd[p] = np.cos(ang)
        sgn = -1.0 if (d % 32) < 16 else 1.0
        sind[p] = sgn * np.sin(ang)
    cst = np.zeros((128, 1024), np.float32)
    cst[:, 0:128] = 1.0
    cst[0:64, 128:192] = 1.0
    cst[64:128, 192:256] = 1.0
    for m in range(128):
        k = m + 16 if (m % 32) < 16 else m - 16
        cst[k, 256 + m] = 1.0
    cst[:, 384:512] = np.eye(128, dtype=np.float32)
    kk = np.arange(128)[:, None]
    qq = np.arange(128)[None, :]
    mp = np.where(kk >= qq, 0.0, -30000.0).astype(np.float32)
    mn = np.where(kk <= qq, 0.0, -30000.0).astype(np.float32)
    cst[:, 512:640] = mp
    cst[:, 640:768] = mp
    cst[:, 768:896] = mn
    cst[:, 896:1024] = mn
    rce = np.zeros((128, 36), np.float32)
    for j in range(2):
        for p in range(128):
            w = ((2, 4), (8, 16))[j][p // 64]
            rce[p, 32 + j] = 1.0 / w
            for side in range(2):
                for e in range(8):
                    if side == 0:
                        t = e
                        cntv = min(t + w // 2, 10 ** 6) - max(t - w // 2, 0)
                    else:
                        dist = 8 - e
                        cntv = min(w // 2, dist) + w // 2
                    rce[p, j * 16 + side * 8 + e] = 1.0 / cntv
    return cosd, sind, cst, rce


def _prep(inputs, L):
    f = lambda a: np.ascontiguousarray(np.asarray(a, dtype=np.float32))
    x = f(inputs["x"]); ctx = f(inputs["ctx"]); c = f(inputs["c"]); c_ctx = f(inputs["c_ctx"])
    w_in = f(inputs["w_in"])[:L]; w_out = f(inputs["w_out"])[:L]
    A0, B0, KV0, C0, D0 = 0, 768, 1024, 1280, 1792
    qa = np.r_[B0 + 0:B0 + 64, B0 + 128:B0 + 192]
    qb = np.r_[B0 + 64:B0 + 128, B0 + 192:B0 + 256]
    kcols = np.r_[KV0:KV0 + 128]
    vcols = np.r_[KV0 + 128:KV0 + 256]
    h0, h1 = np.r_[0:128], np.r_[128:256]
    gb0, gb1 = np.r_[256:384], np.r_[384:512]
    gc0, gc1 = np.r_[512:640], np.r_[640:768]
    u0, u1 = np.r_[C0:C0 + 128], np.r_[C0 + 128:C0 + 256]
    vs = np.r_[C0 + 256:C0 + 512]
    x0, x1 = np.r_[D0:D0 + 128], np.r_[D0 + 128:D0 + 256]
    perm = np.concatenate([qa, qb, kcols, vcols, h0, gc0, gb0, h1, gc1, gb1, u0, u1, vs, x0, x1])
    assert perm.shape[0] == 2048 and len(set(perm.tolist())) == 2048
    w_in_p = np.ascontiguousarray(w_in[:, :, perm])
    ya = np.r_[0:256]
    yb = np.r_[256 + 0:256 + 64, 256 + 128:256 + 192, 256 + 64:256 + 128, 256 + 192:256 + 256]
    yc = np.r_[512:768]
    yd = np.r_[768:1024]
    rperm = np.concatenate([yb, ya, yc, yd])
    w_out_p = np.ascontiguousarray(w_out[:, rperm, :])
    pp = np.zeros((L, 128, NPP), np.float32)
    fm = lambda v, n: v.reshape(n, 128).T
    for l in range(L):
        pp[l, :, 0:48] = fm(f(inputs["b_ada"])[l], 48)
        pp[l, :, 48:56] = fm(f(inputs["norm_mix"])[l], 8)
        pp[l, :, 56:64] = fm(f(inputs["norm_ff"])[l], 8)
        pp[l, :, 64] = np.tile(f(inputs["q_norm"])[l], 2)
        pp[l, :, 65] = np.tile(f(inputs["k_norm"])[l], 2)
        sk = f(inputs["sink"])[l]
        pp[l, 0:64, 66] = sk[0]; pp[l, 64:128, 66] = sk[2]
        pp[l, 0:64, 67] = sk[1]; pp[l, 64:128, 67] = sk[3]
        pp[l, :, 76:80] = sk[None, 0:4]
        pp[l, :, 68:70] = fm(f(inputs["pool_scale"])[l], 2)
        cw = f(inputs["conv_w"])[l]
        for j in range(2):
            for tap in range(3):
                pp[l, :, 70 + j * 3 + tap] = cw[tap, j * 128:(j + 1) * 128]
    sgn = np.ascontiguousarray(np.broadcast_to(f(inputs["sgu_norm"])[:L, None, :], (L, 128, 256)))
    bs = f(inputs["b_sgu"])[:L]
    bst = np.zeros((L, 128, 2, 128), np.float32)
    for j in range(2):
        bst[:, 0:64, j, :] = bs[:, 2 * j, None, :]
        bst[:, 64:128, j, :] = bs[:, 2 * j + 1, None, :]
    bst = bst.reshape(L, 128, 256)
    ws = f(inputs["w_sgu"])[:L]
    wst = np.ascontiguousarray(ws.transpose(0, 3, 1, 2)).reshape(L, 128, 512)
    cosd, sind, cst, rce = _consts()
    shared = {
        "w_ada": f(inputs["w_ada"])[:L], "w_in": w_in_p, "w_out": w_out_p,
        "w_ff1": f(inputs["w_ff1"])[:L], "w_ff2": f(inputs["w_ff2"])[:L],
        "pp": pp, "sgn": sgn, "bst": bst, "wst": wst, "wpool": f(inputs["w_pool"])[:L],
        "cosd": cosd, "sind": sind, "cstd": cst, "rced": rce,
    }
    in_maps = []
    for b in range(8):
        m = dict(shared)
        m["hT0"] = np.ascontiguousarray(np.concatenate([x[b].T, ctx[b].T], axis=1))
        cv = np.zeros((128, 8, 2), np.float32)
        cv[:, :, 0] = c[b].reshape(8, 128).T
        cv[:, :, 1] = c_ctx.reshape(8, 128).T
        m["cvec"] = cv.reshape(128, 16)
        in_maps.append(m)
    return in_maps


def run(inputs, L=DEPTH, trace=False, dbg=None):
    nc = build(L, dbg)
    in_maps = _prep(inputs, L)
    res = run_bass_kernel_spmd(nc, in_maps, core_ids=list(range(8)), trace=trace)
    full = np.stack([np.ascontiguousarray(r["outT"].T) for r in res.results], axis=0).astype(np.float32)
    if dbg is not None:
        return full, res
    return np.ascontiguousarray(full[:, :TL, :]), res


def kernel(**inputs):
    out, _ = run(inputs, DEPTH)
    return out
```
